# Optimizing a Trainium2 kernel written in Bass

```python
import jax, jax.numpy as jnp
from jax import lax
import numpy as np

D_MODEL = 2048
BATCH = 4
SEQ = 2048
DEPTH = 1

D_A = D_MODEL
HGRN_HEAD_DIM = 128
N_HGRN_HEADS = D_A // HGRN_HEAD_DIM
D_B = D_MODEL
CONV_GROUP_DIM = 128
N_CONV_GROUPS = D_B // CONV_GROUP_DIM
CONV_WIDTH = 3
D_MIX = D_A + D_B
CHUNK = 64
PLE_DIM = 256
EPS = 1e-6
IN_SIZES = (D_A, D_A, D_A, D_A, D_A, D_B, D_B, D_B, D_B)
D_IN_TOTAL = sum(IN_SIZES)
IN_SPLITS = tuple(int(s) for s in np.cumsum(IN_SIZES)[:-1])

kernel_name = 'hymba_hgrn2_shortconv_bidir_encoder'


def _rmsnorm(x, w):
    xf = x.astype(jnp.float32)
    xf = xf * lax.rsqrt(jnp.mean(xf * xf, axis=-1, keepdims=True) + EPS)
    return (xf * w.astype(jnp.float32)).astype(x.dtype)


def _head_rmsnorm(o, w):
    b, s, _ = o.shape
    of = o.astype(jnp.float32).reshape(b, s, N_HGRN_HEADS, HGRN_HEAD_DIM)
    of = of * lax.rsqrt(jnp.mean(of * of, axis=-1, keepdims=True) + EPS)
    return (of.reshape(b, s, D_A) * w.astype(jnp.float32)).astype(o.dtype)


def _to_chunks(t):
    b, s, _ = t.shape
    return t.reshape(b, s // CHUNK, CHUNK, N_HGRN_HEADS, HGRN_HEAD_DIM).transpose(0, 3, 1, 2, 4)


def _from_chunks(t):
    b, h, n, c, d = t.shape
    return t.transpose(0, 2, 3, 1, 4).reshape(b, n * c, h * d)


def _gla_chunked(q, k, v, log_f):
    b = jnp.cumsum(log_f, axis=-2)
    b_mid = b[..., CHUNK // 2 - 1:CHUNK // 2, :]
    b_last = b[..., -1:, :]
    q_rel = q * jnp.exp(b - b_mid)
    k_rel = k * jnp.exp(b_mid - b)
    scores = jnp.einsum('bhntk,bhnsk->bhnts', q_rel, k_rel)
    mask = jnp.tril(jnp.ones((CHUNK, CHUNK), dtype=bool))
    scores = jnp.where(mask, scores, 0.0)
    o_intra = jnp.einsum('bhnts,bhnsv->bhntv', scores, v)
    q_in = q * jnp.exp(b)
    k_st = k * jnp.exp(b_last - b)
    kv = jnp.einsum('bhnsk,bhnsv->bhnkv', k_st, v)
    decay = jnp.exp(b_last[..., 0, :])

    def step(state, inp):
        d_n, kv_n = inp
        return d_n[..., None] * state + kv_n, state

    bsz, h = q.shape[0], q.shape[1]
    s0 = jnp.zeros((bsz, h, HGRN_HEAD_DIM, HGRN_HEAD_DIM), q.dtype)
    _, s_prev = lax.scan(step, s0, (jnp.moveaxis(decay, 2, 0), jnp.moveaxis(kv, 2, 0)))
    s_prev = jnp.moveaxis(s_prev, 0, 2)
    o_inter = jnp.einsum('bhntk,bhnkv->bhntv', q_in, s_prev)
    return o_intra + o_inter


def _hgrn2_direction(q, v, f_logit, lb, reverse):
    f = lb + (1.0 - lb) * jax.nn.sigmoid(f_logit)
    log_f = jnp.log(f)
    k = 1.0 - f
    if reverse:
        q, v, k, log_f = (jnp.flip(t, axis=1) for t in (q, v, k, log_f))
    o = _from_chunks(_gla_chunked(_to_chunks(q), _to_chunks(k), _to_chunks(v), _to_chunks(log_f)))
    if reverse:
        o = jnp.flip(o, axis=1)
    return o


def _hgrn2_bidirectional(q, v, f_fwd, f_bwd, lb):
    dt = q.dtype
    q32, v32 = q.astype(jnp.float32), v.astype(jnp.float32)
    o = (_hgrn2_direction(q32, v32, f_fwd.astype(jnp.float32), lb[0], False)
         + _hgrn2_direction(q32, v32, f_bwd.astype(jnp.float32), lb[1], True))
    return o.astype(dt)


def _short_conv(u, w):
    pad = CONV_WIDTH // 2
    return lax.conv_general_dilated(
        u, w[:, None, :].astype(u.dtype), window_strides=(1,), padding=((pad, pad),),
        dimension_numbers=('NWC', 'WIO', 'NWC'), feature_group_count=D_B)


def setup_inputs(seed: int = 0) -> dict:
    key = jax.random.key(seed)
    ks = jax.random.split(key, 13)
    nrm = jax.random.normal
    return {
        'x': nrm(ks[0], (BATCH, SEQ, D_MODEL), jnp.float32),
        'p': nrm(ks[1], (DEPTH, BATCH, SEQ, PLE_DIM), jnp.float32),
        'norm_w': 1.0 + 0.02 * nrm(ks[2], (DEPTH, D_MODEL), jnp.float32),
        'w_in': nrm(ks[3], (DEPTH, D_MODEL, D_IN_TOTAL), jnp.float32) * D_MODEL ** -0.5,
        'lb_theta': 0.1 * nrm(ks[4], (2, DEPTH + 1, D_A), jnp.float32),
        'hgrn_norm_w': 1.0 + 0.02 * nrm(ks[5], (DEPTH, D_A), jnp.float32),
        'conv_w': nrm(ks[6], (DEPTH, CONV_WIDTH, D_B), jnp.float32) * CONV_WIDTH ** -0.5,
        'conv_norm_w': 1.0 + 0.02 * nrm(ks[7], (DEPTH, D_B), jnp.float32),
        'w_out': nrm(ks[8], (DEPTH, D_MIX, D_MODEL), jnp.float32) * D_MIX ** -0.5,
        'w_ple': nrm(ks[9], (DEPTH, PLE_DIM, D_MODEL), jnp.float32) * PLE_DIM ** -0.5,
        'w_ple_gate': nrm(ks[10], (DEPTH, D_MODEL, D_MODEL), jnp.float32) * D_MODEL ** -0.5,
        'final_norm_w': 1.0 + 0.02 * nrm(ks[11], (D_MODEL,), jnp.float32),
    }


def reference(x, p, norm_w, w_in, lb_theta, hgrn_norm_w, conv_w, conv_norm_w,
              w_out, w_ple, w_ple_gate, final_norm_w):
    h = x
    lb_all = jnp.cumsum(jax.nn.softmax(lb_theta.astype(jnp.float32), axis=1), axis=1)
    for i in range(DEPTH):
        hn = _rmsnorm(h, norm_w[i])
        proj = hn @ w_in[i]
        q, v, f_fwd, f_bwd, z_a, b_gate, c_gate, h_b, z_b = jnp.split(proj, IN_SPLITS, axis=-1)
        o_a = _hgrn2_bidirectional(jax.nn.silu(q), v, f_fwd, f_bwd, lb_all[:, i])
        o_a = _head_rmsnorm(o_a, hgrn_norm_w[i]) * jax.nn.silu(z_a)
        y_b = b_gate * _short_conv(c_gate * h_b, conv_w[i])
        o_b = _rmsnorm(y_b, conv_norm_w[i]) * jax.nn.silu(z_b)
        h = h + jnp.concatenate([o_a, o_b], axis=-1) @ w_out[i]
        gate = jax.nn.sigmoid(h @ w_ple_gate[i])
        h = h + (p[i] @ w_ple[i]) * gate
    return _rmsnorm(h, final_norm_w)
```

```python
import numpy as np
from contextlib import ExitStack

import concourse.bass as bass
import concourse.mybir as mybir
from concourse.bass_utils import run_bass_kernel_spmd

F32 = mybir.dt.float32
BF16 = mybir.dt.bfloat16
AF = mybir.ActivationFunctionType
ALU = mybir.AluOpType
AX = mybir.AxisListType

EPS = 1e-6
CH = 64


class Cfg:
    def __init__(self, D=2048, T=1024, PLE=256, debug=False):
        self.D = D
        self.T = T
        self.PLE = PLE
        self.KC = D // 128
        self.NH = D // 128
        self.NT = T // 128
        self.NB = T // 512
        self.debug = debug


class Prog:
    ENG = {'pe': 'tensor', 'act': 'scalar', 'dve': 'vector', 'pool': 'gpsimd', 'sp': 'sync'}

    def __init__(self):
        self.ops = []
        self.lastw = {}
        self.readers = {}
        self.last_eng = {}
        self.last_dma = {}
        self.dma_hist = {}
        self.marks = []
        self.nopfn = {}
        self.fence = None

    def add(self, eng, fn, r=(), w=(), dma=None):
        i = len(self.ops)
        deps = set()
        for x in r:
            lw = self.lastw.get(x)
            if lw is not None:
                deps.add(lw)
        for x in w:
            lw = self.lastw.get(x)
            if lw is not None:
                deps.add(lw)
            deps.update(self.readers.get(x, {}).values())
        rk = dma if dma is not None else eng
        for x in r:
            self.readers.setdefault(x, {})[rk if dma is None else (rk, i)] = i
        for x in w:
            self.lastw[x] = i
            self.readers[x] = {}
        if self.fence is not None:
            deps.add(self.fence)
        deps.discard(i)
        self.ops.append({'eng': eng, 'fn': fn, 'deps': deps, 'dma': dma, 'sig': dma is not None})
        self.last_eng[eng] = i
        if dma is not None:
            self.last_dma[dma] = i
        return i

    def pe(self, fn, r=(), w=()):
        return self.add('pe', fn, r, w)

    def act(self, fn, r=(), w=()):
        return self.add('act', fn, r, w)

    def dve(self, fn, r=(), w=()):
        return self.add('dve', fn, r, w)

    def pool(self, fn, r=(), w=()):
        return self.add('pool', fn, r, w)

    DMA_WIN = 8

    def dma(self, eng, out, in_, r=(), w=(), sem=None):
        lst = self.dma_hist.setdefault(eng, [])
        n = len(lst)
        key = '%s%d' % (eng, n % self.DMA_WIN)
        i = self.add(eng, lambda e: e.dma_start(out=out, in_=in_), r, w, dma=key)
        if n >= self.DMA_WIN:
            self.ops[i]['deps'].add(lst[n - self.DMA_WIN])
        lst.append(i)
        return i

    def barrier(self):
        self.marks.append(len(self.ops))
        targets = set(self.last_eng.values()) | set(self.last_dma.values())
        i = len(self.ops)
        self.ops.append({'eng': 'dve', 'fn': self.nopfn['dve'], 'deps': set(targets), 'dma': None, 'sig': False})
        self.last_eng['dve'] = i
        self.fence = i

    def wait_all(self, eng, res):
        deps = set()
        for x in res:
            lw = self.lastw.get(x)
            if lw is not None:
                deps.add(lw)
        self.ops.append({'eng': eng, 'fn': None, 'deps': deps, 'dma': None, 'sig': False})

    def emit(self, nc, es, upto=None):
        ops = self.ops
        if upto is not None:
            ops = ops[:upto]
            last = {}
            for i, op in enumerate(ops):
                last[op['dma'] if op['dma'] is not None else op['eng']] = i
            for eng in ('pe', 'act', 'dve', 'pool'):
                ops.append({'eng': eng, 'fn': self.nopfn.get(eng), 'deps': set(), 'dma': None, 'sig': False})
            ops.append({'eng': 'sp', 'fn': None, 'deps': set(last.values()), 'dma': None, 'sig': False})
            self.ops = ops
        for op in ops:
            op['waits'] = []
        for op in ops:
            for d in op['deps']:
                p = ops[d]
                if p['dma'] is None and p['eng'] == 'pe' and op['eng'] == 'pe':
                    continue
                p['sig'] = True
                op['waits'].append(d)
        cnt = {}
        for op in ops:
            if op['sig']:
                key = op['dma'] if op['dma'] is not None else op['eng']
                inc = 16 if op['dma'] is not None else 1
                cnt[key] = cnt.get(key, 0) + inc
                op['cnt'] = cnt[key]
                op['key'] = key
        sems = {}
        for key in cnt:
            sems[key] = es.enter_context(nc.semaphore("s_" + str(key)))
        blk = es.enter_context(nc.Block())
        for ename, attr in self.ENG.items():
            mine = [op for op in ops if op['eng'] == ename]
            if not mine:
                continue

            def body(e, mine=mine):
                waited = {}
                for op in mine:
                    need = {}
                    for d in op['waits']:
                        p = ops[d]
                        need[p['key']] = max(need.get(p['key'], 0), p['cnt'])
                    for k, v in need.items():
                        if waited.get(k, 0) < v:
                            e.wait_ge(sems[k], v)
                            waited[k] = v
                    if op['fn'] is not None:
                        ins = op['fn'](e)
                        if op['sig']:
                            ins.then_inc(sems[op['key']], 16 if op['dma'] is not None else 1)

            getattr(blk, attr)(body)


def interleave(gens):
    gens = list(gens)
    while gens:
        nxt = []
        for g in gens:
            try:
                next(g)
                nxt.append(g)
            except StopIteration:
                pass
        gens = nxt


def build(cfg):
    D, T, KC, NH, NT, NB, PLE = cfg.D, cfg.T, cfg.KC, cfg.NH, cfg.NT, cfg.NB, cfg.PLE
    T2 = 2 * T
    NT2 = 2 * NT
    KP = PLE // 128
    NG = 9
    NOB = D // 256
    NGB = D // 512
    nc = bass.Bass("TRN2", target_bir_lowering=False)

    def din(name, shape):
        return nc.dram_tensor(name, list(shape), F32, kind="ExternalInput").ap()

    xl = din("xl", [T2, D])
    pl = din("pl", [T, PLE])
    wu_in = din("wu_in", [NG * NH, 128, KC * 128])
    wu_out = din("wu_out", [2 * NGB, 128, KC * 512])
    wu_g = din("wu_g", [NGB, 128, KC * 512])
    wu_p = din("wu_p", [NGB, 128, KP * 512])
    chvec = din("chvec", [128, 9 * NH])
    bcv = din("bcv", [2, 128, D])
    cst = din("cst", [5, 128, 512])
    y = nc.dram_tensor("y", [T, D], F32, kind="ExternalOutput").ap()
    dbg = {}
    if cfg.debug:
        for nm, shp in (("d_hnT", [128, KC * T2]), ("d_sinit", [128, NH * 128]),
                        ("d_omix", [128, 2 * KC * T]), ("d_h2", [128, NT * D])):
            dbg[nm] = nc.dram_tensor(nm, shp, F32, kind="ExternalOutput").ap()

    P = Prog()
    es = ExitStack()

    TOTW = nc.sbuf_bytes_remaining // 4 - 64
    big = es.enter_context(nc.sbuf_tensor("big", [128, TOTW], F32))
    ps = [es.enter_context(nc.psum_tensor("ps%d" % i, [128, 512], F32))[:, :] for i in range(8)]
    psb = [p_.bitcast(BF16) for p_ in ps]

    class Carve:
        def __init__(self, start=0):
            self.off = start

        def f32(self, n):
            o = self.off
            self.off += n
            assert self.off <= TOTW, ("SBUF overflow", self.off, TOTW)
            return big[:, o:o + n]

        def bf(self, n):
            w = (n + 1) // 2
            o = self.off
            self.off += w
            assert self.off <= TOTW, ("SBUF overflow", self.off, TOTW)
            return big[:, o:o + w].bitcast(BF16)

    cv = Carve()
    ident = cv.bf(128)
    mask_o = cv.bf(512)
    mask_i = cv.bf(512)
    rmask = cv.f32(512)
    ones_f = cv.f32(512)
    cstage = cv.f32(512)
    chv = cv.f32(9 * NH)
    lbv = cv.f32(2 * NH)
    omlv = cv.f32(2 * NH)
    nomlv = cv.f32(2 * NH)
    epsv = cv.f32(1)
    mcol = cv.f32(2)
    sinit = cv.f32(NH * 128)
    rstdB = cv.f32(NT)
    small = cv.f32(64)
    halo = cv.bf(KC * 2)
    regB = cv.bf(KC * T)
    omixA = cv.bf(KC * T)
    phase4_base = cv.off
    hnT_own = cv.bf(KC * T)
    phase_base = cv.off

    def hn_own(kc, a, b):
        return hnT_own[:, kc * T + a: kc * T + b]

    def hn_par(kc, a, b):
        return regB[:, kc * T + a: kc * T + b]

    def hn_any(kc, a, b):
        if a >= T:
            return hn_par(kc, a - T, b - T)
        return hn_own(kc, a, b)

    def hn_res(a):
        return 'hnT_par' if a >= T else 'hnT_own'

    P.nopfn = {
        'pe': lambda e: e.matmul(ps[7][:, 0:2], lhsT=ident, rhs=ident[:, 0:2], start=True, stop=True),
        'act': lambda e: e.activation(out=small[:, 62:63], in_=epsv, func=AF.Copy),
        'dve': lambda e: e.memset(small[:, 63:64], 0.0),
        'pool': lambda e: e.memset(small[:, 59:60], 0.0),
    }
    dmac = [0]

    def wdma(out, in_, r, w, key):
        P.dma('pool', out, in_, r=r, w=w, sem=key)

    def act_sigmoid(out_ap, in_ap, rres, wres):
        P.act(lambda e: e.activation(out=out_ap, in_=in_ap, func=AF.Exp, scale=-1.0), r=list(rres), w=list(wres))
        P.act(lambda e: e.activation(out=out_ap, in_=out_ap, func=AF.Ln, bias=ones_f[0:out_ap.shape[0], 0:1], scale=1.0),
              r=list(wres) + ['ones_f'], w=list(wres))
        P.act(lambda e: e.activation(out=out_ap, in_=out_ap, func=AF.Exp, scale=-1.0), r=list(wres), w=list(wres))

    P.dma('sp', cstage, cst[0], w=['cstage'], sem='c0')
    P.dve(lambda e: e.tensor_copy(out=ident, in_=cstage[:, 0:128]), r=['cstage'], w=['ident'])
    P.dma('sp', cstage, cst[1], r=[], w=['cstage'], sem='c0')
    P.dve(lambda e: e.tensor_copy(out=mask_o, in_=cstage), r=['cstage'], w=['mask_o'])
    P.dma('sp', cstage, cst[2], w=['cstage'], sem='c0')
    P.dve(lambda e: e.tensor_copy(out=mask_i, in_=cstage), r=['cstage'], w=['mask_i'])
    P.dma('sp', rmask, cst[3], w=['rmask'], sem='c1')
    P.dma('sp', ones_f, cst[4], w=['ones_f'], sem='c1')
    P.dma('sp', chv, chvec, w=['chv'], sem='c1')
    P.dve(lambda e: e.memset(epsv, EPS), w=['epsv'])
    P.dve(lambda e: e.memset(mcol[0:64, 0:1], 1.0), w=['mcol'])
    P.dve(lambda e: e.memset(mcol[64:128, 0:1], 0.0), w=['mcol'])
    P.dve(lambda e: e.memset(mcol[0:64, 1:2], 0.0), w=['mcol'])
    P.dve(lambda e: e.memset(mcol[64:128, 1:2], 1.0), w=['mcol'])
    for d in range(2):
        t0 = chv[:, (2 * d) * NH:(2 * d + 1) * NH]
        t1 = chv[:, (2 * d + 1) * NH:(2 * d + 2) * NH]
        lo = lbv[:, d * NH:(d + 1) * NH]
        oo = omlv[:, d * NH:(d + 1) * NH]
        sm = small[:, d * NH:(d + 1) * NH] if 2 * NH <= 64 else None
        P.dve(lambda e, t0=t0, t1=t1, sm=sm: e.tensor_tensor(out=sm, in0=t0, in1=t1, op=ALU.subtract),
              r=['chv'], w=['small'])
        act_sigmoid(lo, sm, ['small'], ['lbv'])
        P.dve(lambda e, lo=lo, oo=oo: e.tensor_scalar(out=oo, in0=lo, scalar1=-1.0, scalar2=1.0,
                                                     op0=ALU.mult, op1=ALU.add), r=['lbv'], w=['omlv'])
        no = nomlv[:, d * NH:(d + 1) * NH]
        P.dve(lambda e, lo=lo, no=no: e.tensor_scalar(out=no, in0=lo, scalar1=1.0, scalar2=-1.0,
                                                     op0=ALU.mult, op1=ALU.add), r=['lbv'], w=['nomlv'])

    def chcol(idx, h):
        return chv[:, idx * NH + h: idx * NH + h + 1]

    def rstd_from_ss(ss_ap, out_ap, n, rres, wres, tmp_ap):
        P.act(lambda e: e.activation(out=tmp_ap, in_=ss_ap, func=AF.Ln, bias=epsv, scale=1.0 / n),
              r=list(rres) + ['epsv'], w=['rs_tmp'])
        P.act(lambda e: e.activation(out=out_ap, in_=tmp_ap, func=AF.Exp, scale=-0.5),
              r=['rs_tmp'], w=list(wres))

    c1 = Carve(phase_base)
    nwbc = c1.f32(D)
    xb = [c1.f32(D) for _ in range(2)]
    sqb = c1.f32(D)
    hnb = [c1.bf(D) for _ in range(2)]
    ssb = c1.f32(4)
    P.dma('sp', nwbc, bcv[0], w=['nwbc'], sem='c1')
    for tt in range(NT2):
        s = tt % 2
        xs, hs = xb[s], hnb[s]
        P.dma('sp', xs, xl[tt * 128:(tt + 1) * 128, :], w=['xb%d' % s], sem='x%d' % s)
        P.act(lambda e, xs=xs: e.activation(out=sqb, in_=xs, func=AF.Square), r=['xb%d' % s], w=['sqb'])
        ss1 = ssb[:, s:s + 1]
        rs1 = ssb[:, 2 + s:3 + s]
        P.dve(lambda e, ss1=ss1: e.tensor_reduce(out=ss1, in_=sqb, axis=AX.X, op=ALU.add), r=['sqb'], w=['ss%d' % s])
        rstd_from_ss(ss1, rs1, D, ['ss%d' % s], ['rs%d' % s], small[:, 60:61])
        P.dve(lambda e, xs=xs, hs=hs, rs1=rs1: e.scalar_tensor_tensor(out=hs, in0=xs, scalar=rs1, in1=nwbc,
                                                                     op0=ALU.mult, op1=ALU.mult),
              r=['xb%d' % s, 'rs%d' % s, 'nwbc'], w=['hnb%d' % s])
        for g in range((KC + 7) // 8):
            bank = 2 * s + (g % 2)
            n = min(8, KC - 8 * g)
            for j in range(n):
                kc = 8 * g + j
                P.pe(lambda e, bank=bank, j=j, kc=kc, hs=hs: e.transpose(
                    out=psb[bank][:, j * 128:(j + 1) * 128], in_=hs[:, kc * 128:(kc + 1) * 128], identity=ident),
                    r=['hnb%d' % s, 'ident'], w=['ps%d' % bank])
            a = tt * 128
            base = hnT_own if a < T else regB
            aa = a if a < T else a - T
            dst = base.rearrange("p (k t) -> p k t", t=T)[:, 8 * g:8 * g + n, aa:aa + 128]
            src = psb[bank][:, 0:n * 128].rearrange("p (k t) -> p k t", t=128)
            eng = P.act if g % 2 == 0 else P.dve
            if g % 2 == 0:
                P.act(lambda e, dst=dst, src=src: e.activation(out=dst, in_=src, func=AF.Copy),
                      r=['ps%d' % bank], w=[hn_res(a)])
            else:
                P.dve(lambda e, dst=dst, src=src: e.tensor_copy(out=dst, in_=src),
                      r=['ps%d' % bank], w=[hn_res(a)])
    P.dve(lambda e: e.tensor_copy(out=halo.rearrange("p (k c) -> p k c", c=2),
                                  in_=regB.rearrange("p (k t) -> p k t", t=T)[:, :, 0:2]),
          r=['hnT_par'], w=['halo'])
    if cfg.debug:
        P.dma('pool', dbg['d_hnT'][:, 0:KC * T], hnT_own, r=['hnT_own'], w=['dbg0'], sem='dbg')
        P.dma('pool', dbg['d_hnT'][:, KC * T:2 * KC * T], regB, r=['hnT_par'], w=['dbg0'], sem='dbg')
    P.barrier()

    def proj_fm(bank, wslot, wres, col0, tok0, ncols=512):
        gi_ = col0 // 128
        for kc in range(KC):
            lh = wslot[:, (gi_ * KC + kc) * 128:(gi_ * KC + kc + 1) * 128]
            rh = hn_any(kc, tok0, tok0 + ncols)
            P.pe(lambda e, kc=kc, lh=lh, rh=rh: e.matmul(ps[bank][:, 0:ncols], lhsT=lh, rhs=rh,
                                                         start=(kc == 0), stop=(kc == KC - 1)),
                 r=[wres, hn_res(tok0)], w=['ps%d' % bank])

    wslot_w = [0]

    c2 = Carve(phase_base)
    wslot_w[0] = 256
    wB = [c2.bf(KC * 256) for _ in range(2)]
    b_t2 = [c2.f32(512) for _ in range(2)]
    b_lfp = [c2.f32(1 + T) for _ in range(2)]
    b_kk = [c2.f32(T) for _ in range(2)]
    b_e = [c2.f32(T) for _ in range(2)]
    b_ones = c2.f32(T)
    b_kw = [c2.bf(T) for _ in range(2)]
    b_kwtok = [c2.bf(T) for _ in range(2)]
    b_vtok = [c2.bf(T) for _ in range(2)]
    for q_ in range(2):
        P.dve(lambda e, q_=q_: e.memset(b_lfp[q_][:, 0:1], 0.0), w=['b_lfp%d' % q_])
    P.dve(lambda e: e.memset(b_ones, 1.0), w=['b_ones'])

    def b0_load(h):
        s = h % 2
        for gi, g in enumerate((1, 3)):
            wdma(wB[s][:, gi * KC * 128:(gi + 1) * KC * 128], wu_in[g * NH + h], r=[], w=['wB%d' % s], key='wB%d' % s)

    def b0_head(h):
        s = h % 2
        w_ = wB[s]
        wres = 'wB%d' % s
        lb1 = lbv[:, NH + h:NH + h + 1]
        om1 = omlv[:, NH + h:NH + h + 1]
        t2, lfp, kk_, e_, kw, kwtok, vtk = b_t2[s], b_lfp[s], b_kk[s], b_e[s], b_kw[s], b_kwtok[s], b_vtok[s]
        for tb in range(NB):
            bank = tb % 2
            proj_fm(bank, w_, wres, 128, T + tb * 512)
            tq = b_t2[tb % 2] if False else t2
            act_sigmoid(t2, ps[bank], ['ps%d' % bank], ['b_t2%d' % s])
            P.dve(lambda e: e.tensor_scalar(out=t2, in0=t2, scalar1=om1, scalar2=lb1, op0=ALU.mult, op1=ALU.add),
                  r=['b_t2%d' % s, 'omlv', 'lbv'], w=['b_t2%d' % s])
            P.act(lambda e, tb=tb: e.activation(out=lfp[:, 1 + tb * 512:1 + (tb + 1) * 512], in_=t2, func=AF.Ln),
                  r=['b_t2%d' % s], w=['b_lfp%d' % s])
            P.pool(lambda e, tb=tb: e.tensor_scalar(out=kk_[:, tb * 512:(tb + 1) * 512], in0=t2, scalar1=-1.0, scalar2=1.0,
                                                    op0=ALU.mult, op1=ALU.add), r=['b_t2%d' % s], w=['b_kk%d' % s])
        for g in range((NT + 3) // 4):
            n = min(4, NT - 4 * g)
            vb = 4 + (g % 2)
            for j in range(n):
                tt = 4 * g + j
                for kc in range(KC):
                    P.pe(lambda e, j=j, tt=tt, kc=kc, vb=vb: e.matmul(ps[vb][:, j * 128:(j + 1) * 128],
                                                                      lhsT=hn_par(kc, tt * 128, (tt + 1) * 128),
                                                                      rhs=w_[:, kc * 128:(kc + 1) * 128],
                                                                      start=(kc == 0), stop=(kc == KC - 1)),
                         r=[wres, 'hnT_par'], w=['ps%d' % vb])
            P.act(lambda e, g=g, n=n, vb=vb: e.activation(out=vtk[:, g * 512:g * 512 + n * 128], in_=ps[vb][:, 0:n * 128],
                                                          func=AF.Copy), r=['ps%d' % vb], w=['b_vtok%d' % s])
        P.dve(lambda e: e.tensor_tensor_scan(out=e_, data0=lfp[:, 0:T], data1=b_ones, initial=0.0,
                                             op0=ALU.add, op1=ALU.mult), r=['b_lfp%d' % s, 'b_ones'], w=['b_e%d' % s])
        P.act(lambda e: e.activation(out=e_, in_=e_, func=AF.Exp), r=['b_e%d' % s], w=['b_e%d' % s])
        P.dve(lambda e: e.tensor_tensor(out=kw, in0=kk_, in1=e_, op=ALU.mult), r=['b_kk%d' % s, 'b_e%d' % s], w=['b_kw%d' % s])
        tbk = 2 + s
        for g in range(NT // 8 if NT >= 8 else 1):
            n = min(8, NT - 8 * g)
            for j in range(n):
                tt = 8 * g + j
                P.pe(lambda e, j=j, tt=tt: e.transpose(out=psb[tbk][:, j * 128:(j + 1) * 128],
                                                       in_=kw[:, tt * 128:(tt + 1) * 128], identity=ident),
                     r=['b_kw%d' % s, 'ident'], w=['ps%d' % tbk])
            P.dve(lambda e, g=g, n=n: e.tensor_copy(out=kwtok[:, g * 1024:g * 1024 + n * 128], in_=psb[tbk][:, 0:n * 128]),
                  r=['ps%d' % tbk], w=['b_kwtok%d' % s])
        sb = 6 + s
        for tt in range(NT):
            P.pe(lambda e, tt=tt: e.matmul(ps[sb][:, 0:128], lhsT=kwtok[:, tt * 128:(tt + 1) * 128],
                                           rhs=vtk[:, tt * 128:(tt + 1) * 128], start=(tt == 0), stop=(tt == NT - 1)),
                 r=['b_kwtok%d' % s, 'b_vtok%d' % s], w=['ps%d' % sb])
        P.dve(lambda e: e.tensor_copy(out=sinit[:, h * 128:(h + 1) * 128], in_=ps[sb][:, 0:128]), r=['ps%d' % sb], w=['sinit'])

    b0_load(0)
    if NH > 1:
        b0_load(1)
    for h in range(NH):
        b0_head(h)
        if h + 2 < NH:
            b0_load(h + 2)
    if cfg.debug:
        P.dma('sp', dbg['d_sinit'], sinit, r=['sinit'], w=['dbg1'], sem='dbg1')
    P.barrier()

    c3 = Carve(phase_base)
    wslot_w[0] = 640
    wA = [c3.bf(KC * 640) for _ in range(2)]
    cB = Carve(phase_base)
    so_words = 0

    class RB:
        off = 0

    def rb_bf(n):
        if 12 * T > KC * T:
            return c3.bf(n)
        o = RB.off
        RB.off += n
        assert RB.off <= KC * T
        return regB[:, o:o + n]

    qrel = [[rb_bf(T) for _ in range(2)] for _ in range(2)]
    krelT = [[rb_bf(T) for _ in range(2)] for _ in range(2)]
    vtok = [rb_bf(T) for _ in range(2)]
    zs = [rb_bf(T) for _ in range(2)]
    NCK = T // CH
    scal = [[c3.f32(3 * NCK) for _ in range(2)] for _ in range(2)]
    a_qs = c3.f32(512)
    a_qs2 = [a_qs, cstage]
    rbrem = regB.bitcast(F32)
    if 12 * T <= KC * T and (KC * T - 12 * T) // 2 >= 4 * 512:
        _o = 12 * T // 2
        xtra = [rbrem[:, _o + i * 512:_o + (i + 1) * 512] for i in range(4)]
    else:
        xtra = [c3.f32(512) for _ in range(4)]
    a_t2 = [c3.f32(512), xtra[0]]
    a_t3 = [c3.f32(512), xtra[1]]
    a_lfp = [c3.f32(513), c3.f32(513)]
    a_c = [c3.f32(512), xtra[2]]
    a_a = [c3.f32(512), xtra[3]]
    a_qm = c3.f32(512)
    a_km = c3.f32(512)
    a_tt = c3.f32(2 * 8)
    g_ktok = [c3.bf(T) for _ in range(2)]
    g_ktokh = [c3.bf(T) for _ in range(2)]
    g_scm = [c3.bf(T) for _ in range(2)]
    KVR = min(8, NCK)
    g_kvs = [c3.f32(KVR * 128) for _ in range(2)]
    g_sbf = [c3.bf(NCK * 128) for _ in range(2)]
    g_S4 = [c3.f32(128) for _ in range(4)]
    g_osq = c3.f32(512)
    g_rs = c3.f32(512)
    g_on = g_osq
    g_ln = g_rs
    for q_ in range(2):
        P.dve(lambda e, q_=q_: e.memset(a_lfp[q_][:, 0:1], 0.0), w=['a_lfp%d' % q_])

    def a_load(h):
        s = h % 2
        for g in range(5):
            wdma(wA[s][:, g * KC * 128:(g + 1) * KC * 128], wu_in[g * NH + h], r=[], w=['wA%d' % s], key='wA%d_%d' % (s, g % 2))

    def a_proj(h):
        s = h % 2
        w_ = wA[s]
        wres = 'wA%d' % s
        pend = [None]
        for tb in range(NB):
            t0 = tb * 512
            nck = 512 // CH
            qs_ = a_qs2[tb % 2]
            qsr = 'a_qs%d' % (tb % 2)
            proj_fm(0, w_, wres, 0, t0)
            act_sigmoid(qs_, ps[0], ['ps0'], [qsr])
            P.dve(lambda e, qs_=qs_: e.tensor_tensor(out=qs_, in0=ps[0], in1=qs_, op=ALU.mult), r=['ps0', qsr], w=[qsr])
            proj_fm(1, w_, wres, 4 * 128, t0)
            act_sigmoid(a_km, ps[1], ['ps1'], ['a_km'])
            P.dve(lambda e, t0=t0: e.tensor_tensor(out=zs[s][:, t0:t0 + 512], in0=ps[1], in1=a_km, op=ALU.mult),
                  r=['ps1', 'a_km'], w=['zs%d' % s])
            yield
            def it_body(d, tb=tb, t0=t0, nck=nck, qs_=qs_, qsr=qsr):
                bank = d % 2
                lb1 = lbv[:, d * NH + h:d * NH + h + 1]
                om1 = omlv[:, d * NH + h:d * NH + h + 1]
                proj_fm(bank, w_, wres, (2 + d) * 128, t0)
                nom1 = nomlv[:, d * NH + h:d * NH + h + 1]
                t2_, t3_, lfp_, c_, aa_ = a_t2[d], a_t3[d], a_lfp[d], a_c[d], a_a[d]
                act_sigmoid(t2_, ps[bank], ['ps%d' % bank], ['a_t2%d' % d])
                P.act(lambda e, t2_=t2_, lfp_=lfp_, lb1=lb1, om1=om1: e.activation(out=lfp_[:, 1:513], in_=t2_, func=AF.Ln,
                                                                                   bias=lb1, scale=om1),
                      r=['a_t2%d' % d, 'lbv', 'omlv'], w=['a_lfp%d' % d])
                P.pool(lambda e, t2_=t2_, t3_=t3_, nom1=nom1, om1=om1: e.tensor_scalar(out=t3_, in0=t2_, scalar1=nom1, scalar2=om1,
                                                                                     op0=ALU.mult, op1=ALU.add),
                       r=['a_t2%d' % d, 'omlv', 'nomlv'], w=['a_t3%d' % d])
                if d == 0:
                    P.dve(lambda e, c_=c_, lfp_=lfp_: e.tensor_tensor_scan(out=c_, data0=rmask, data1=lfp_[:, 1:513], initial=0.0,
                                                                         op0=ALU.mult, op1=ALU.add),
                          r=['rmask', 'a_lfp%d' % d], w=['a_c%d' % d])
                    ref = 31
                else:
                    P.dve(lambda e, c_=c_, lfp_=lfp_: e.tensor_tensor_scan(out=c_, data0=lfp_[:, 0:512], data1=rmask, initial=0.0,
                                                                         op0=ALU.add, op1=ALU.mult),
                          r=['rmask', 'a_lfp%d' % d], w=['a_c%d' % d])
                    ref = 32
                c3v = c_.rearrange("p (n c) -> p n c", c=CH)
                a3v = aa_.rearrange("p (n c) -> p n c", c=CH)
                P.dve(lambda e, c3v=c3v, a3v=a3v, ref=ref: e.tensor_tensor(
                    out=a3v, in0=c3v, in1=c3v[:, :, ref:ref + 1].to_broadcast([128, nck, CH]), op=ALU.subtract),
                    r=['a_c%d' % d], w=['a_a%d' % d])
                yield
                sc_ = scal[s][d]
                dec = sc_[:, 0 * NCK + tb * nck:0 * NCK + (tb + 1) * nck]
                gg = sc_[:, 1 * NCK + tb * nck:1 * NCK + (tb + 1) * nck]
                scc = sc_[:, 2 * NCK + tb * nck:2 * NCK + (tb + 1) * nck]
                sres = 'scal%d_%d' % (s, d)
                c_last = c3v[:, :, CH - 1:CH].rearrange("p n o -> p (n o)")
                c_ref = c3v[:, :, ref:ref + 1].rearrange("p n o -> p (n o)")
                a_last = a3v[:, :, CH - 1:CH].rearrange("p n o -> p (n o)")
                if d == 0:
                    P.act(lambda e, dec=dec, c_last=c_last: e.activation(out=dec, in_=c_last, func=AF.Exp),
                          r=['a_c%d' % d], w=[sres])
                    P.act(lambda e, scc=scc, c_ref=c_ref: e.activation(out=scc, in_=c_ref, func=AF.Exp),
                          r=['a_c%d' % d], w=[sres])
                    P.act(lambda e, gg=gg, a_last=a_last: e.activation(out=gg, in_=a_last, func=AF.Exp),
                          r=['a_a%d' % d], w=[sres])
                else:
                    lf_last = lfp_[:, 1:513].rearrange("p (n c) -> p n c", c=CH)[:, :, CH - 1:CH].rearrange("p n o -> p (n o)")
                    tt_ = a_tt[:, 0:nck]
                    tt2 = a_tt[:, 8:8 + nck]
                    P.dve(lambda e, tt_=tt_, c_last=c_last, lf_last=lf_last: e.tensor_tensor(
                        out=tt_, in0=c_last, in1=lf_last, op=ALU.add), r=['a_c%d' % d, 'a_lfp%d' % d], w=['a_tt'])
                    P.dve(lambda e, tt_=tt_, tt2=tt2, c_ref=c_ref: e.tensor_tensor(
                        out=tt2, in0=tt_, in1=c_ref, op=ALU.subtract), r=['a_c%d' % d, 'a_tt'], w=['a_tt2'])
                    P.act(lambda e, dec=dec, tt_=tt_: e.activation(out=dec, in_=tt_, func=AF.Exp), r=['a_tt'], w=[sres])
                    P.act(lambda e, gg=gg, c_ref=c_ref: e.activation(out=gg, in_=c_ref, func=AF.Exp), r=['a_c%d' % d], w=[sres])
                    P.act(lambda e, scc=scc, tt2=tt2: e.activation(out=scc, in_=tt2, func=AF.Exp), r=['a_tt2'], w=[sres])
                sq = 1.0 if d == 0 else -1.0
                P.act(lambda e, sq=sq, aa_=aa_: e.activation(out=a_qm, in_=aa_, func=AF.Exp, scale=sq), r=['a_a%d' % d], w=['a_qm'])
                P.act(lambda e, sq=sq, aa_=aa_: e.activation(out=a_km, in_=aa_, func=AF.Exp, scale=-sq), r=['a_a%d' % d], w=['a_km'])
                P.dve(lambda e, d=d, t0=t0: e.tensor_tensor(out=qrel[s][d][:, t0:t0 + 512], in0=qs_, in1=a_qm, op=ALU.mult),
                      r=[qsr, 'a_qm'], w=['qrel%d_%d' % (s, d)])
                P.pool(lambda e, d=d, t0=t0, t3_=t3_: e.tensor_tensor(out=krelT[s][d][:, t0:t0 + 512], in0=t3_, in1=a_km, op=ALU.mult),
                       r=['a_t3%d' % d, 'a_km'], w=['krelT%d_%d' % (s, d)])
            for d in range(2):
                g_ = it_body(d)
                next(g_)
                if pend[0] is not None:
                    for _ in pend[0]:
                        pass
                pend[0] = g_
                yield
            for j in range(4):
                tt = tb * 4 + j
                for kc in range(KC):
                    P.pe(lambda e, j=j, tt=tt, kc=kc: e.matmul(ps[2][:, j * 128:(j + 1) * 128],
                                                               lhsT=hn_own(kc, tt * 128, (tt + 1) * 128),
                                                               rhs=w_[:, (KC + kc) * 128:(KC + kc + 1) * 128],
                                                               start=(kc == 0), stop=(kc == KC - 1)),
                         r=[wres, 'hnT_own'], w=['ps2'])
            P.dve(lambda e, t0=t0: e.tensor_copy(out=vtok[s][:, t0:t0 + 512], in_=ps[2]), r=['ps2'], w=['vtok%d' % s])
            yield
        if pend[0] is not None:
            for _ in pend[0]:
                pass
        yield


    def a_gla(h):
        s = h % 2
        masks = (mask_o, mask_i)
        for d in range(2):
            kT = krelT[s][d]
            qr = qrel[s][d]
            sres = 'scal%d_%d' % (s, d)
            sc_ = scal[s][d]
            if d == 0:
                P.dve(lambda e: e.memset(g_S4[0], 0.0), w=['g_S0'])
            else:
                P.dve(lambda e: e.tensor_copy(out=g_S4[0], in_=sinit[:, h * 128:(h + 1) * 128]), r=['sinit'], w=['g_S0'])
            si = [0]
            gorder = list(range(NT // 4)) if d == 0 else list(range(NT // 4 - 1, -1, -1))
            for g in gorder:
                for j in range(4):
                    tt = 4 * g + j
                    P.pe(lambda e, j=j, tt=tt, kT=kT: e.transpose(out=psb[3][:, j * 128:(j + 1) * 128],
                                                                  in_=kT[:, tt * 128:(tt + 1) * 128], identity=ident),
                         r=['krelT%d_%d' % (s, d), 'ident'], w=['ps3'])
                P.dve(lambda e, g=g, d=d: e.tensor_scalar(out=g_ktok[d][:, g * 512:(g + 1) * 512], in0=psb[3][:, 0:512],
                                                          scalar1=mcol[:, 0:1], scalar2=None, op0=ALU.mult),
                      r=['ps3', 'mcol'], w=['g_ktok%d' % d])
                P.dve(lambda e, g=g, d=d: e.tensor_scalar(out=g_ktokh[d][:, g * 512:(g + 1) * 512], in0=psb[3][:, 0:512],
                                                          scalar1=mcol[:, 1:2], scalar2=None, op0=ALU.mult),
                      r=['ps3', 'mcol'], w=['g_ktok%d' % d])
                for j in range(4):
                    tt = 4 * g + j
                    P.pe(lambda e, j=j, tt=tt, kT=kT, qr=qr: e.matmul(ps[4][:, j * 128:(j + 1) * 128],
                                                                      lhsT=kT[:, tt * 128:(tt + 1) * 128],
                                                                      rhs=qr[:, tt * 128:(tt + 1) * 128], start=True, stop=True),
                         r=['krelT%d_%d' % (s, d), 'qrel%d_%d' % (s, d)], w=['ps4'])
                P.dve(lambda e, g=g, d=d: e.tensor_tensor(out=g_scm[d][:, g * 512:(g + 1) * 512], in0=ps[4], in1=masks[d],
                                                          op=ALU.mult), r=['ps4', 'mask_o', 'mask_i'], w=['g_scm%d' % d])
                yield
                for half in range(2):
                    for jj in range(4):
                        n = 8 * g + 4 * half + jj
                        tt = n // 2
                        r0 = (n % 2) * 64
                        P.pe(lambda e, jj=jj, tt=tt, r0=r0, d=d: e.matmul(
                            ps[5][:, jj * 128:(jj + 1) * 128],
                            lhsT=(g_ktok[d] if r0 == 0 else g_ktokh[d])[:, tt * 128:(tt + 1) * 128],
                            rhs=vtok[s][:, tt * 128:(tt + 1) * 128], start=True, stop=True),
                            r=['g_ktok%d' % d, 'vtok%d' % s], w=['ps5'])
                    n0 = 8 * g + 4 * half
                    sl0 = (n0 % KVR) * 128
                    gsl = sc_[:, NCK + n0:NCK + n0 + 4]
                    P.dve(lambda e, d=d, sl0=sl0, gsl=gsl: e.tensor_tensor(
                        out=g_kvs[d][:, sl0:sl0 + 512].rearrange("p (n c) -> p n c", c=128),
                        in0=ps[5].rearrange("p (n c) -> p n c", c=128),
                        in1=gsl.unsqueeze(2).to_broadcast([128, 4, 128]), op=ALU.mult),
                        r=['ps5', sres], w=['g_kvs%d_%d' % (d, (n0 + q_) % KVR) for q_ in range(4)])
                yield
                order = range(8 * g, 8 * g + 8) if d == 0 else range(8 * g + 7, 8 * g - 1, -1)
                for n in order:
                    cur = si[0] % 4
                    nx = (si[0] + 1) % 4
                    si[0] += 1
                    P.act(lambda e, n=n, d=d, sc_=sc_, cur=cur: e.activation(
                        out=g_sbf[d][:, n * 128:(n + 1) * 128], in_=g_S4[cur], func=AF.Identity,
                        scale=sc_[:, 2 * NCK + n:2 * NCK + n + 1]), r=['g_S%d' % cur, sres], w=['g_sbf%d' % d])
                    P.dve(lambda e, n=n, d=d, sc_=sc_, cur=cur, nx=nx: e.scalar_tensor_tensor(
                        out=g_S4[nx], in0=g_S4[cur], scalar=sc_[:, n:n + 1], in1=g_kvs[d][:, (n % KVR) * 128:(n % KVR + 1) * 128],
                        op0=ALU.mult, op1=ALU.add), r=['g_S%d' % cur, 'g_kvs%d_%d' % (d, n % KVR), sres], w=['g_S%d' % nx])
                yield
        for g in range(NT // 4):
            for j in range(4):
                tt = 4 * g + j
                mm = []
                for d in range(2):
                    mm.append((vtok[s][:, tt * 128:(tt + 1) * 128], g_scm[d][:, tt * 128:(tt + 1) * 128],
                               slice(j * 128, (j + 1) * 128)))
                for c in range(2):
                    n = 2 * tt + c
                    for d in range(2):
                        mm.append((g_sbf[d][:, n * 128:(n + 1) * 128], qrel[s][d][:, n * CH:(n + 1) * CH],
                                   slice(j * 128 + c * CH, j * 128 + (c + 1) * CH)))
                for i_, (l_, r_, sl_) in enumerate(mm):
                    P.pe(lambda e, l_=l_, r_=r_, sl_=sl_, i_=i_, last=(i_ == len(mm) - 1): e.matmul(
                        ps[6][:, sl_], lhsT=l_, rhs=r_, start=(i_ == 0), stop=last),
                        r=['vtok%d' % s, 'g_scm0', 'g_scm1', 'g_sbf0', 'g_sbf1', 'qrel%d_0' % s, 'qrel%d_1' % s], w=['ps6'])
            P.act(lambda e: e.activation(out=g_osq, in_=ps[6], func=AF.Square), r=['ps6'], w=['g_osq'])
            P.pe(lambda e: e.matmul(ps[7], lhsT=ones_f[:, 0:128], rhs=g_osq, start=True, stop=True),
                 r=['ones_f', 'g_osq'], w=['ps7'])
            P.act(lambda e: e.activation(out=g_ln, in_=ps[7], func=AF.Ln, bias=epsv, scale=1.0 / 128), r=['ps7', 'epsv'], w=['g_rs'])
            P.act(lambda e: e.activation(out=g_rs, in_=g_ln, func=AF.Exp, scale=-0.5), r=['g_rs'], w=['g_rs'])
            P.dve(lambda e: e.tensor_tensor(out=g_on, in0=ps[6], in1=g_rs, op=ALU.mult), r=['ps6', 'g_rs'], w=['g_osq'])
            P.dve(lambda e, g=g: e.scalar_tensor_tensor(out=omixA[:, h * T + g * 512:h * T + (g + 1) * 512], in0=g_on,
                                                        scalar=chcol(4, h), in1=zs[s][:, g * 512:(g + 1) * 512],
                                                        op0=ALU.mult, op1=ALU.mult),
                  r=['g_osq', 'chv', 'zs%d' % s], w=['omixA'])
            yield

    a_load(0)
    if NH > 1:
        a_load(1)
    for h in range(NH + 1):
        gens = []
        if h < NH:
            gens.append(a_proj(h))
        if h >= 1:
            gens.append(a_gla(h - 1))
        interleave(gens)
        if h + 2 < NH:
            a_load(h + 2)
    P.barrier()

    c4 = Carve(phase_base)
    wslot_w[0] = 512
    wC = [c4.bf(KC * 512) for _ in range(2)]
    omixB = regB
    c_cs = c4.f32(512)
    c_u = c4.f32(T + 2)
    c_y = c4.f32(512)
    c_yb = c4.f32(512)
    c_sq = c4.f32(512)
    c_z = c4.f32(512)
    c_h = c4.f32(2)
    c_ss = c4.f32(NT)
    P.dve(lambda e: e.memset(c_ss, 0.0), w=['c_ss'])
    P.dve(lambda e: e.memset(c_u[:, 0:1], 0.0), w=['c_u'])

    def c_load(cb):
        s = cb % 2
        for g in range(4):
            wdma(wC[s][:, g * KC * 128:(g + 1) * KC * 128], wu_in[(5 + g) * NH + cb], r=[], w=['wC%d' % s], key='wC%d_%d' % (s, g % 2))

    def c_block(cb):
        s = cb % 2
        w_ = wC[s]
        wres = 'wC%d' % s
        for tb in range(NB):
            t0 = tb * 512
            proj_fm(0, w_, wres, 1 * 128, t0)
            P.act(lambda e: e.activation(out=c_cs, in_=ps[0], func=AF.Copy), r=['ps0'], w=['c_cs'])
            proj_fm(1, w_, wres, 2 * 128, t0)
            P.dve(lambda e, t0=t0: e.tensor_tensor(out=c_u[:, 1 + t0:1 + t0 + 512], in0=c_cs, in1=ps[1], op=ALU.mult),
                  r=['c_cs', 'ps1'], w=['c_u'])
        for gi, g in enumerate((1, 2)):
            for kc in range(KC):
                P.pe(lambda e, gi=gi, g=g, kc=kc: e.matmul(ps[2][:, gi * 2:gi * 2 + 2], lhsT=w_[:, (g * KC + kc) * 128:(g * KC + kc + 1) * 128],
                                                           rhs=halo[:, kc * 2:kc * 2 + 2], start=(kc == 0), stop=(kc == KC - 1)),
                     r=[wres, 'halo'], w=['ps2'])
        P.act(lambda e: e.activation(out=c_h, in_=ps[2][:, 0:2], func=AF.Copy), r=['ps2'], w=['c_h'])
        P.dve(lambda e: e.tensor_tensor(out=c_u[:, T + 1:T + 2], in0=c_h[:, 0:1], in1=ps[2][:, 2:3], op=ALU.mult),
              r=['c_h', 'ps2'], w=['c_u'])
        for tb in range(NB):
            t0 = tb * 512
            P.dve(lambda e, t0=t0: e.tensor_scalar(out=c_y, in0=c_u[:, t0:t0 + 512], scalar1=chcol(5, cb), scalar2=None,
                                                   op0=ALU.mult), r=['c_u', 'chv'], w=['c_y'])
            P.dve(lambda e, t0=t0: e.scalar_tensor_tensor(out=c_y, in0=c_u[:, t0 + 1:t0 + 513], scalar=chcol(6, cb), in1=c_y,
                                                          op0=ALU.mult, op1=ALU.add), r=['c_u', 'chv', 'c_y'], w=['c_y'])
            P.dve(lambda e, t0=t0: e.scalar_tensor_tensor(out=c_y, in0=c_u[:, t0 + 2:t0 + 514], scalar=chcol(7, cb), in1=c_y,
                                                          op0=ALU.mult, op1=ALU.add), r=['c_u', 'chv', 'c_y'], w=['c_y'])
            proj_fm(3, w_, wres, 0, t0)
            P.dve(lambda e: e.tensor_tensor(out=c_yb, in0=c_y, in1=ps[3], op=ALU.mult), r=['c_y', 'ps3'], w=['c_yb'])
            P.act(lambda e: e.activation(out=c_sq, in_=c_yb, func=AF.Square), r=['c_yb'], w=['c_sq'])
            for j in range(4):
                tt = tb * 4 + j
                P.pe(lambda e, j=j, tt=tt: e.matmul(ps[7][:, tt:tt + 1], lhsT=c_sq[:, j * 128:(j + 1) * 128],
                                                    rhs=ones_f[:, 0:1], start=True, stop=True),
                     r=['c_sq', 'ones_f'], w=['ps7'])
            P.dve(lambda e, tb=tb: e.tensor_tensor(out=c_ss[:, tb * 4:tb * 4 + 4], in0=c_ss[:, tb * 4:tb * 4 + 4],
                                                   in1=ps[7][:, tb * 4:tb * 4 + 4], op=ALU.add), r=['ps7', 'c_ss'], w=['c_ss'])
            proj_fm(4, w_, wres, 3 * 128, t0)
            act_sigmoid(c_z, ps[4], ['ps4'], ['c_z'])
            P.dve(lambda e: e.tensor_tensor(out=c_z, in0=ps[4], in1=c_z, op=ALU.mult), r=['ps4', 'c_z'], w=['c_z'])
            P.dve(lambda e, t0=t0: e.scalar_tensor_tensor(out=omixB[:, cb * T + t0:cb * T + t0 + 512], in0=c_yb,
                                                          scalar=chcol(8, cb), in1=c_z, op0=ALU.mult, op1=ALU.mult),
                  r=['c_yb', 'chv', 'c_z'], w=['omixB'])

    c_load(0)
    if NH > 1:
        c_load(1)
    for cb in range(NH):
        c_block(cb)
        if cb + 2 < NH:
            c_load(cb + 2)
    rstd_from_ss(c_ss, rstdB, D, ['c_ss'], ['rstdB'], small[:, 32:32 + NT])
    if cfg.debug:
        P.dma('pool', dbg['d_omix'][:, 0:KC * T], omixA, r=['omixA'], w=['dbg2'], sem='dbg')
        P.dma('pool', dbg['d_omix'][:, KC * T:2 * KC * T], omixB, r=['omixB'], w=['dbg2'], sem='dbg')
    P.barrier()

    c5 = Carve(phase4_base)
    h2 = c5.f32(NT * D)
    wO = [c5.bf(KC * 512) for _ in range(3)]
    for tt in range(NT):
        P.dma('sp', h2[:, tt * D:(tt + 1) * D], xl[tt * 128:(tt + 1) * 128, :], w=['h2_%d' % tt], sem='xh%d' % (tt % 4))
    ounits = [(j, part) for j in range(NGB) for part in range(2)]

    def o_load(u):
        s = u % 3
        half = (KC // 2) * 512 if KC >= 2 else KC * 512
        wdma(wO[s][:, 0:half], wu_out[u][:, 0:half], r=[], w=['wO%d' % s], key='wO%d' % s)
        if half < KC * 512:
            wdma(wO[s][:, half:KC * 512], wu_out[u][:, half:KC * 512], r=[], w=['wO%d' % s], key='wO%d' % s)

    def o_block(j):
        sa = (2 * j) % 3
        sb_ = (2 * j + 1) % 3
        for tt in range(NT):
            ba = 0 + (tt % 2)
            bb = 2 + (tt % 2)
            for kc in range(KC):
                P.pe(lambda e, kc=kc, ba=ba, tt=tt: e.matmul(ps[ba], lhsT=omixA[:, kc * T + tt * 128:kc * T + (tt + 1) * 128],
                                                             rhs=wO[sa][:, kc * 512:(kc + 1) * 512], start=(kc == 0), stop=(kc == KC - 1)),
                     r=['omixA', 'wO%d' % sa], w=['ps%d' % ba])
            for kc in range(KC):
                P.pe(lambda e, kc=kc, bb=bb, tt=tt: e.matmul(ps[bb], lhsT=omixB[:, kc * T + tt * 128:kc * T + (tt + 1) * 128],
                                                             rhs=wO[sb_][:, kc * 512:(kc + 1) * 512], start=(kc == 0), stop=(kc == KC - 1)),
                     r=['omixB', 'wO%d' % sb_], w=['ps%d' % bb])
            hv = h2[:, tt * D + j * 512:tt * D + (j + 1) * 512]
            P.dve(lambda e, hv=hv, ba=ba: e.tensor_tensor(out=hv, in0=hv, in1=ps[ba], op=ALU.add),
                  r=['ps%d' % ba, 'h2_%d' % tt], w=['h2_%d' % tt])
            P.dve(lambda e, hv=hv, bb=bb, tt=tt: e.scalar_tensor_tensor(out=hv, in0=ps[bb], scalar=rstdB[:, tt:tt + 1], in1=hv,
                                                                        op0=ALU.mult, op1=ALU.add),
                  r=['ps%d' % bb, 'h2_%d' % tt, 'rstdB'], w=['h2_%d' % tt])

    nxt_o = [0]

    def o_prefetch(upto):
        while nxt_o[0] < min(upto, len(ounits)):
            o_load(nxt_o[0])
            nxt_o[0] += 1

    o_prefetch(3)
    for j in range(NGB):
        o_block(j)
        o_prefetch((j + 1) * 2 + 3)
    if cfg.debug:
        P.dma('sp', dbg['d_h2'], h2, r=['h2_%d' % tt for tt in range(NT)], w=['dbg3'], sem='dbg1')
    P.barrier()

    c6 = Carve(c5.off - 3 * (KC * 256))
    h2T = omixA
    pT = regB[:, 0:KP * T]
    rbf = regB.bitcast(F32)
    if KP * T // 2 + 2 * D <= KC * T // 2:
        e_ob = [rbf[:, KP * T // 2 + i * D:KP * T // 2 + (i + 1) * D] for i in range(2)]
    else:
        e_ob = [c6.f32(D) for _ in range(2)]
    KH = max(1, KC // 2)
    NKH = KC // KH
    wG = [c6.bf(KH * 512) for _ in range(3)]
    wP = [c6.bf(KP * 512) for _ in range(2)]
    fnbc = c6.f32(D)
    e_hb = [c6.bf(D) for _ in range(2)]
    e_sq = e_hb[0].bitcast(F32) if False else None
    e_sqbase = c6.off - D
    e_sq = big[:, e_sqbase:e_sqbase + D]
    e_pb = c6.f32(PLE)
    e_pbb = c6.bf(PLE)
    e_gs = [c6.f32(512) for _ in range(2)]
    e_t = [c6.f32(512) for _ in range(2)]
    e_ss = c6.f32(4)
    P.dma('sp', fnbc, bcv[1], w=['fnbc'], sem='c1')
    for tt in range(NT):
        s = tt % 2
        P.act(lambda e, tt=tt, s=s: e.activation(out=e_hb[s], in_=h2[:, tt * D:(tt + 1) * D], func=AF.Copy),
              r=['h2_%d' % tt], w=['e_hb%d' % s])
        for g in range((KC + 7) // 8):
            bank = 2 * s + (g % 2)
            n = min(8, KC - 8 * g)
            for j in range(n):
                kc = 8 * g + j
                P.pe(lambda e, bank=bank, j=j, kc=kc, s=s: e.transpose(out=psb[bank][:, j * 128:(j + 1) * 128],
                                                                       in_=e_hb[s][:, kc * 128:(kc + 1) * 128], identity=ident),
                     r=['e_hb%d' % s, 'ident'], w=['ps%d' % bank])
            dst = h2T.rearrange("p (k t) -> p k t", t=T)[:, 8 * g:8 * g + n, tt * 128:(tt + 1) * 128]
            src_ = psb[bank][:, 0:n * 128].rearrange("p (k t) -> p k t", t=128)
            P.dve(lambda e, dst=dst, src_=src_: e.tensor_copy(out=dst, in_=src_), r=['ps%d' % bank], w=['h2T'])
        P.dma('sp', e_pb, pl[tt * 128:(tt + 1) * 128, :], w=['e_pb'], sem='pp')
        P.act(lambda e: e.activation(out=e_pbb, in_=e_pb, func=AF.Copy), r=['e_pb'], w=['e_pbb'])
        for kc in range(KP):
            P.pe(lambda e, kc=kc: e.transpose(out=psb[4][:, kc * 128:(kc + 1) * 128], in_=e_pbb[:, kc * 128:(kc + 1) * 128],
                                              identity=ident), r=['e_pbb', 'ident'], w=['ps4'])
        dst = pT.rearrange("p (k t) -> p k t", t=T)[:, 0:KP, tt * 128:(tt + 1) * 128]
        src_ = psb[4][:, 0:KP * 128].rearrange("p (k t) -> p k t", t=128)
        P.dve(lambda e, dst=dst, src_=src_: e.tensor_copy(out=dst, in_=src_), r=['ps4'], w=['pT'])

    gunits = [(j, kh) for j in range(NGB) for kh in range(NKH)]

    def g_load(u):
        j, kh = gunits[u]
        s = u % 3
        wdma(wG[s], wu_g[j][:, kh * KH * 512:(kh + 1) * KH * 512], r=[], w=['wG%d' % s], key='wG%d' % s)
        if kh == 0:
            wdma(wP[j % 2], wu_p[j], r=[], w=['wP%d' % (j % 2)], key='wP%d' % (j % 2))

    def g_block(j):
        for tt in range(NT):
            b1 = 5 + (tt % 2)
            b2 = 0 + (tt % 2)
            for kc in range(KC):
                u = j * NKH + kc // KH
                s = u % 3
                kk_ = kc % KH
                P.pe(lambda e, kc=kc, b1=b1, tt=tt, s=s, kk_=kk_: e.matmul(
                    ps[b1], lhsT=h2T[:, kc * T + tt * 128:kc * T + (tt + 1) * 128],
                    rhs=wG[s][:, kk_ * 512:(kk_ + 1) * 512], start=(kc == 0), stop=(kc == KC - 1)),
                    r=['h2T', 'wG%d' % s], w=['ps%d' % b1])
            sp_ = j % 2
            for kc in range(KP):
                P.pe(lambda e, kc=kc, b2=b2, tt=tt, sp_=sp_: e.matmul(
                    ps[b2], lhsT=pT[:, kc * T + tt * 128:kc * T + (tt + 1) * 128],
                    rhs=wP[sp_][:, kc * 512:(kc + 1) * 512], start=(kc == 0), stop=(kc == KP - 1)),
                    r=['pT', 'wP%d' % sp_], w=['ps%d' % b2])
            q_ = tt % 2
            act_sigmoid(e_gs[q_], ps[b1], ['ps%d' % b1], ['e_gs%d' % q_])
            P.dve(lambda e, b2=b2, q_=q_: e.tensor_tensor(out=e_t[q_], in0=ps[b2], in1=e_gs[q_], op=ALU.mult),
                  r=['ps%d' % b2, 'e_gs%d' % q_], w=['e_t%d' % q_])
            hv = h2[:, tt * D + j * 512:tt * D + (j + 1) * 512]
            P.pool(lambda e, hv=hv, q_=q_: e.tensor_tensor(out=hv, in0=hv, in1=e_t[q_], op=ALU.add),
                   r=['e_t%d' % q_, 'h2_%d' % tt], w=['h2_%d' % tt])

    nxt_u = [0]

    def g_prefetch(upto):
        while nxt_u[0] < min(upto, len(gunits)):
            g_load(nxt_u[0])
            nxt_u[0] += 1

    g_prefetch(3)
    for j in range(NGB):
        g_block(j)
        g_prefetch((j + 1) * NKH + 3)
    for tt in range(NT):
        s = tt % 2
        hv = h2[:, tt * D:(tt + 1) * D]
        P.act(lambda e, hv=hv: e.activation(out=e_sq, in_=hv, func=AF.Square), r=['h2_%d' % tt], w=['e_sq', 'e_hb0', 'e_hb1'])
        ss1 = e_ss[:, s:s + 1]
        rs1 = e_ss[:, 2 + s:3 + s]
        P.dve(lambda e, ss1=ss1: e.tensor_reduce(out=ss1, in_=e_sq, axis=AX.X, op=ALU.add), r=['e_sq'], w=['e_ss%d' % s])
        rstd_from_ss(ss1, rs1, D, ['e_ss%d' % s], ['e_rs%d' % s], small[:, 61:62])
        P.dve(lambda e, hv=hv, rs1=rs1, s=s: e.scalar_tensor_tensor(out=e_ob[s], in0=hv, scalar=rs1, in1=fnbc,
                                                                    op0=ALU.mult, op1=ALU.mult),
              r=['h2_%d' % tt, 'e_rs%d' % s, 'fnbc'], w=['e_ob%d' % s])
        P.dma('sp', y[tt * 128:(tt + 1) * 128, :], e_ob[s], r=['e_ob%d' % s], w=['yout%d' % tt], sem='yo%d' % s)
    P.wait_all('sp', ['yout%d' % tt for tt in range(NT)] + ['dbg0', 'dbg1', 'dbg2', 'dbg3'])

    stop = getattr(cfg, 'stop', None)
    P.emit(nc, es, upto=(P.marks[stop] if stop is not None and stop < len(P.marks) else (stop if stop is not None and stop >= 100 else None)))
    es.close()
    return nc


def _units(w, ncols_unit):
    K, N = w.shape
    kc = K // 128
    nu = N // ncols_unit
    a = w.reshape(kc, 128, nu, ncols_unit).transpose(2, 1, 0, 3)
    return np.ascontiguousarray(a).reshape(nu, 128, kc * ncols_unit)


def _chan(v, nh):
    return np.ascontiguousarray(v.reshape(nh, 128).T)


def _consts():
    c = np.zeros((5, 128, 512), np.float32)
    c[0, :, 0:128] = np.eye(128, dtype=np.float32)
    s = np.arange(128)[:, None]
    t = np.arange(128)[None, :]
    same = (s // CH) == (t // CH)
    mo = (same & (s <= t)).astype(np.float32)
    mi = (same & (s >= t)).astype(np.float32)
    c[1] = np.tile(mo, (1, 4))
    c[2] = np.tile(mi, (1, 4))
    rm = np.ones(512, np.float32)
    rm[::CH] = 0.0
    c[3] = rm[None, :]
    c[4] = 1.0
    return c


def make_in_maps(cfg, x, p, norm_w, w_in, lb_theta, hgrn_norm_w, conv_w, conv_norm_w,
                 w_out, w_ple, w_ple_gate, final_norm_w):
    D, T, NH = cfg.D, cfg.T, cfg.NH
    B = x.shape[0]
    assert x.shape[1] == 2 * T
    f32 = np.float32
    w_in0 = np.asarray(w_in[0], f32)
    groups = [w_in0[:, g * D:(g + 1) * D] for g in range(9)]
    wu = {}
    for par in range(2):
        order = [0, 1, 2, 3, 4, 5, 6, 7, 8] if par == 0 else [0, 1, 3, 2, 4, 5, 6, 7, 8]
        wu[par] = np.concatenate([_units(groups[g], 128) for g in order], axis=0)
    wo = np.asarray(w_out[0], f32)
    uA = _units(wo[:D], 512)
    uB = _units(wo[D:], 512)
    wu_out = np.ascontiguousarray(np.stack([uA, uB], axis=1).reshape(2 * uA.shape[0], 128, -1))
    wu_g = _units(np.asarray(w_ple_gate[0], f32), 512)
    wu_p = _units(np.asarray(w_ple[0], f32), 512)
    bcv = np.stack([np.broadcast_to(np.asarray(norm_w[0], f32), (128, D)),
                    np.broadcast_to(np.asarray(final_norm_w, f32), (128, D))]).astype(f32)
    bcv = np.ascontiguousarray(bcv)
    cst = _consts()
    chv = {}
    for par in range(2):
        do, di = (0, 1) if par == 0 else (1, 0)
        cw = np.asarray(conv_w[0], f32)
        taps = [cw[0], cw[1], cw[2]] if par == 0 else [cw[2], cw[1], cw[0]]
        vecs = [lb_theta[do, 0], lb_theta[do, 1], lb_theta[di, 0], lb_theta[di, 1], hgrn_norm_w[0],
                taps[0], taps[1], taps[2], conv_norm_w[0]]
        chv[par] = np.ascontiguousarray(np.concatenate([_chan(np.asarray(v, f32), NH) for v in vecs], axis=1))
    in_maps = []
    for c in range(2 * B):
        b, par = c // 2, c % 2
        xb = np.asarray(x[b], f32)
        pb = np.asarray(p[0, b], f32)
        if par == 1:
            xb = xb[::-1]
            pb = pb[::-1]
        in_maps.append({
            "xl": np.ascontiguousarray(xb),
            "pl": np.ascontiguousarray(pb[:T]),
            "wu_in": wu[par], "wu_out": wu_out, "wu_g": wu_g, "wu_p": wu_p,
            "chvec": chv[par], "bcv": bcv, "cst": cst,
        })
    return in_maps


def assemble(cfg, results, B):
    T, D = cfg.T, cfg.D
    out = np.empty((B, 2 * T, D), np.float32)
    for c in range(2 * B):
        b, par = c // 2, c % 2
        yc = np.asarray(results[c]["y"], np.float32)
        if par == 0:
            out[b, :T] = yc
        else:
            out[b, T:] = yc[::-1]
    return out


_NC_CACHE = {}


def kernel(x, p, norm_w, w_in, lb_theta, hgrn_norm_w, conv_w, conv_norm_w,
           w_out, w_ple, w_ple_gate, final_norm_w):
    x = np.asarray(x)
    B, S, D = x.shape
    cfg = Cfg(D=D, T=S // 2, PLE=np.asarray(p).shape[-1])
    in_maps = make_in_maps(cfg, x, np.asarray(p), np.asarray(norm_w), np.asarray(w_in), np.asarray(lb_theta),
                           np.asarray(hgrn_norm_w), np.asarray(conv_w), np.asarray(conv_norm_w),
                           np.asarray(w_out), np.asarray(w_ple), np.asarray(w_ple_gate), np.asarray(final_norm_w))
    nc = build(cfg)
    res = run_bass_kernel_spmd(nc, in_maps, core_ids=list(range(2 * B)))
    return assemble(cfg, res.results, B)
```

```python
import numpy as np
from contextlib import ExitStack

import concourse.bass as bass
import concourse.mybir as mybir
from concourse.bass_utils import run_bass_kernel_spmd

F32 = mybir.dt.float32
BF16 = mybir.dt.bfloat16
AF = mybir.ActivationFunctionType
ALU = mybir.AluOpType
AX = mybir.AxisListType

EPS = 1e-6
CH = 64


class Cfg:
    def __init__(self, D=2048, T=1024, PLE=256, debug=False):
        self.D = D
        self.T = T
        self.PLE = PLE
        self.KC = D // 128
        self.NH = D // 128
        self.NT = T // 128
        self.NB = T // 512
        self.debug = debug


class Prog:
    ENG = {'pe': 'tensor', 'act': 'scalar', 'dve': 'vector', 'pool': 'gpsimd', 'sp': 'sync'}

    def __init__(self):
        self.ops = []
        self.lastw = {}
        self.readers = {}
        self.last_eng = {}
        self.last_dma = {}
        self.dma_hist = {}
        self.marks = []
        self.nopfn = {}
        self.fence = None

    def add(self, eng, fn, r=(), w=(), dma=None):
        i = len(self.ops)
        deps = set()
        for x in r:
            lw = self.lastw.get(x)
            if lw is not None:
                deps.add(lw)
        for x in w:
            lw = self.lastw.get(x)
            if lw is not None:
                deps.add(lw)
            deps.update(self.readers.get(x, {}).values())
        rk = dma if dma is not None else eng
        for x in r:
            self.readers.setdefault(x, {})[rk if dma is None else (rk, i)] = i
        for x in w:
            self.lastw[x] = i
            self.readers[x] = {}
        if self.fence is not None:
            deps.add(self.fence)
        deps.discard(i)
        self.ops.append({'eng': eng, 'fn': fn, 'deps': deps, 'dma': dma, 'sig': dma is not None})
        self.last_eng[eng] = i
        if dma is not None:
            self.last_dma[dma] = i
        return i

    def pe(self, fn, r=(), w=()):
        return self.add('pe', fn, r, w)

    def act(self, fn, r=(), w=()):
        return self.add('act', fn, r, w)

    def dve(self, fn, r=(), w=()):
        return self.add('dve', fn, r, w)

    def pool(self, fn, r=(), w=()):
        return self.add('pool', fn, r, w)

    DMA_WIN = 8

    def dma(self, eng, out, in_, r=(), w=(), sem=None):
        lst = self.dma_hist.setdefault(eng, [])
        n = len(lst)
        key = '%s%d' % (eng, n % self.DMA_WIN)
        i = self.add(eng, lambda e: e.dma_start(out=out, in_=in_), r, w, dma=key)
        if n >= self.DMA_WIN:
            self.ops[i]['deps'].add(lst[n - self.DMA_WIN])
        lst.append(i)
        return i

    def barrier(self):
        self.marks.append(len(self.ops))
        targets = set(self.last_eng.values()) | set(self.last_dma.values())
        i = len(self.ops)
        self.ops.append({'eng': 'dve', 'fn': self.nopfn['dve'], 'deps': set(targets), 'dma': None, 'sig': False})
        self.last_eng['dve'] = i
        self.fence = i

    def wait_all(self, eng, res):
        deps = set()
        for x in res:
            lw = self.lastw.get(x)
            if lw is not None:
                deps.add(lw)
        self.ops.append({'eng': eng, 'fn': None, 'deps': deps, 'dma': None, 'sig': False})

    def emit(self, nc, es, upto=None):
        ops = self.ops
        if upto is not None:
            ops = ops[:upto]
            last = {}
            for i, op in enumerate(ops):
                last[op['dma'] if op['dma'] is not None else op['eng']] = i
            for eng in ('pe', 'act', 'dve', 'pool'):
                ops.append({'eng': eng, 'fn': self.nopfn.get(eng), 'deps': set(), 'dma': None, 'sig': False})
            ops.append({'eng': 'sp', 'fn': None, 'deps': set(last.values()), 'dma': None, 'sig': False})
            self.ops = ops
        for op in ops:
            op['waits'] = []
        for op in ops:
            for d in op['deps']:
                p = ops[d]
                if p['dma'] is None and p['eng'] == 'pe' and op['eng'] == 'pe':
                    continue
                p['sig'] = True
                op['waits'].append(d)
        cnt = {}
        for op in ops:
            if op['sig']:
                key = op['dma'] if op['dma'] is not None else op['eng']
                inc = 16 if op['dma'] is not None else 1
                cnt[key] = cnt.get(key, 0) + inc
                op['cnt'] = cnt[key]
                op['key'] = key
        sems = {}
        for key in cnt:
            sems[key] = es.enter_context(nc.semaphore("s_" + str(key)))
        blk = es.enter_context(nc.Block())
        for ename, attr in self.ENG.items():
            mine = [op for op in ops if op['eng'] == ename]
            if not mine:
                continue

            def body(e, mine=mine):
                waited = {}
                for op in mine:
                    need = {}
                    for d in op['waits']:
                        p = ops[d]
                        need[p['key']] = max(need.get(p['key'], 0), p['cnt'])
                    for k, v in need.items():
                        if waited.get(k, 0) < v:
                            e.wait_ge(sems[k], v)
                            waited[k] = v
                    if op['fn'] is not None:
                        ins = op['fn'](e)
                        if op['sig']:
                            ins.then_inc(sems[op['key']], 16 if op['dma'] is not None else 1)

            getattr(blk, attr)(body)


def interleave(gens):
    gens = list(gens)
    while gens:
        nxt = []
        for g in gens:
            try:
                next(g)
                nxt.append(g)
            except StopIteration:
                pass
        gens = nxt


def build(cfg):
    D, T, KC, NH, NT, NB, PLE = cfg.D, cfg.T, cfg.KC, cfg.NH, cfg.NT, cfg.NB, cfg.PLE
    T2 = 2 * T
    NT2 = 2 * NT
    KP = PLE // 128
    NG = 9
    NOB = D // 256
    NGB = D // 512
    nc = bass.Bass("TRN2", target_bir_lowering=False)

    def din(name, shape):
        return nc.dram_tensor(name, list(shape), F32, kind="ExternalInput").ap()

    xl = din("xl", [T2, D])
    pl = din("pl", [T, PLE])
    wu_in = din("wu_in", [NG * NH, 128, KC * 128])
    wu_out = din("wu_out", [2 * NGB, 128, KC * 512])
    wu_g = din("wu_g", [NGB, 128, KC * 512])
    wu_p = din("wu_p", [NGB, 128, KP * 512])
    chvec = din("chvec", [128, 9 * NH])
    bcv = din("bcv", [2, 128, D])
    cst = din("cst", [5, 128, 512])
    y = nc.dram_tensor("y", [T, D], F32, kind="ExternalOutput").ap()
    dbg = {}
    if cfg.debug:
        for nm, shp in (("d_hnT", [128, KC * T2]), ("d_sinit", [128, NH * 128]),
                        ("d_omix", [128, 2 * KC * T]), ("d_h2", [128, NT * D])):
            dbg[nm] = nc.dram_tensor(nm, shp, F32, kind="ExternalOutput").ap()

    P = Prog()
    es = ExitStack()

    TOTW = nc.sbuf_bytes_remaining // 4 - 64
    big = es.enter_context(nc.sbuf_tensor("big", [128, TOTW], F32))
    ps = [es.enter_context(nc.psum_tensor("ps%d" % i, [128, 512], F32))[:, :] for i in range(8)]
    psb = [p_.bitcast(BF16) for p_ in ps]

    class Carve:
        def __init__(self, start=0):
            self.off = start

        def f32(self, n):
            o = self.off
            self.off += n
            assert self.off <= TOTW, ("SBUF overflow", self.off, TOTW)
            return big[:, o:o + n]

        def bf(self, n):
            w = (n + 1) // 2
            o = self.off
            self.off += w
            assert self.off <= TOTW, ("SBUF overflow", self.off, TOTW)
            return big[:, o:o + w].bitcast(BF16)

    cv = Carve()
    ident = cv.bf(128)
    mask_o = cv.bf(512)
    mask_i = cv.bf(512)
    rmask = cv.f32(512)
    ones_f = cv.f32(512)
    cstage = cv.f32(512)
    chv = cv.f32(9 * NH)
    lbv = cv.f32(2 * NH)
    omlv = cv.f32(2 * NH)
    nomlv = cv.f32(2 * NH)
    epsv = cv.f32(1)
    mcol = cv.f32(2)
    sinit = cv.f32(NH * 128)
    rstdB = cv.f32(NT)
    small = cv.f32(64)
    halo = cv.bf(KC * 2)
    regB = cv.bf(KC * T)
    omixA = cv.bf(KC * T)
    phase4_base = cv.off
    hnT_own = cv.bf(KC * T)
    phase_base = cv.off

    def hn_own(kc, a, b):
        return hnT_own[:, kc * T + a: kc * T + b]

    def hn_par(kc, a, b):
        return regB[:, kc * T + a: kc * T + b]

    def hn_any(kc, a, b):
        if a >= T:
            return hn_par(kc, a - T, b - T)
        return hn_own(kc, a, b)

    def hn_res(a):
        return 'hnT_par' if a >= T else 'hnT_own'

    P.nopfn = {
        'pe': lambda e: e.matmul(ps[7][:, 0:2], lhsT=ident, rhs=ident[:, 0:2], start=True, stop=True),
        'act': lambda e: e.activation(out=small[:, 62:63], in_=epsv, func=AF.Copy),
        'dve': lambda e: e.memset(small[:, 63:64], 0.0),
        'pool': lambda e: e.memset(small[:, 59:60], 0.0),
    }
    dmac = [0]

    def wdma(out, in_, r, w, key):
        P.dma('pool', out, in_, r=r, w=w, sem=key)

    def act_sigmoid(out_ap, in_ap, rres, wres):
        P.act(lambda e: e.activation(out=out_ap, in_=in_ap, func=AF.Exp, scale=-1.0), r=list(rres), w=list(wres))
        P.act(lambda e: e.activation(out=out_ap, in_=out_ap, func=AF.Ln, bias=ones_f[0:out_ap.shape[0], 0:1], scale=1.0),
              r=list(wres) + ['ones_f'], w=list(wres))
        P.act(lambda e: e.activation(out=out_ap, in_=out_ap, func=AF.Exp, scale=-1.0), r=list(wres), w=list(wres))

    P.dma('sp', cstage, cst[0], w=['cstage'], sem='c0')
    P.dve(lambda e: e.tensor_copy(out=ident, in_=cstage[:, 0:128]), r=['cstage'], w=['ident'])
    P.dma('sp', cstage, cst[1], r=[], w=['cstage'], sem='c0')
    P.dve(lambda e: e.tensor_copy(out=mask_o, in_=cstage), r=['cstage'], w=['mask_o'])
    P.dma('sp', cstage, cst[2], w=['cstage'], sem='c0')
    P.dve(lambda e: e.tensor_copy(out=mask_i, in_=cstage), r=['cstage'], w=['mask_i'])
    P.dma('sp', rmask, cst[3], w=['rmask'], sem='c1')
    P.dma('sp', ones_f, cst[4], w=['ones_f'], sem='c1')
    P.dma('sp', chv, chvec, w=['chv'], sem='c1')
    P.dve(lambda e: e.memset(epsv, EPS), w=['epsv'])
    P.dve(lambda e: e.memset(mcol[0:64, 0:1], 1.0), w=['mcol'])
    P.dve(lambda e: e.memset(mcol[64:128, 0:1], 0.0), w=['mcol'])
    P.dve(lambda e: e.memset(mcol[0:64, 1:2], 0.0), w=['mcol'])
    P.dve(lambda e: e.memset(mcol[64:128, 1:2], 1.0), w=['mcol'])
    for d in range(2):
        t0 = chv[:, (2 * d) * NH:(2 * d + 1) * NH]
        t1 = chv[:, (2 * d + 1) * NH:(2 * d + 2) * NH]
        lo = lbv[:, d * NH:(d + 1) * NH]
        oo = omlv[:, d * NH:(d + 1) * NH]
        sm = small[:, d * NH:(d + 1) * NH] if 2 * NH <= 64 else None
        P.dve(lambda e, t0=t0, t1=t1, sm=sm: e.tensor_tensor(out=sm, in0=t0, in1=t1, op=ALU.subtract),
              r=['chv'], w=['small'])
        act_sigmoid(lo, sm, ['small'], ['lbv'])
        P.dve(lambda e, lo=lo, oo=oo: e.tensor_scalar(out=oo, in0=lo, scalar1=-1.0, scalar2=1.0,
                                                     op0=ALU.mult, op1=ALU.add), r=['lbv'], w=['omlv'])
        no = nomlv[:, d * NH:(d + 1) * NH]
        P.dve(lambda e, lo=lo, no=no: e.tensor_scalar(out=no, in0=lo, scalar1=1.0, scalar2=-1.0,
                                                     op0=ALU.mult, op1=ALU.add), r=['lbv'], w=['nomlv'])

    def chcol(idx, h):
        return chv[:, idx * NH + h: idx * NH + h + 1]

    def rstd_from_ss(ss_ap, out_ap, n, rres, wres, tmp_ap):
        P.act(lambda e: e.activation(out=tmp_ap, in_=ss_ap, func=AF.Ln, bias=epsv, scale=1.0 / n),
              r=list(rres) + ['epsv'], w=['rs_tmp'])
        P.act(lambda e: e.activation(out=out_ap, in_=tmp_ap, func=AF.Exp, scale=-0.5),
              r=['rs_tmp'], w=list(wres))

    c1 = Carve(phase_base)
    nwbc = c1.f32(D)
    NXB = 4
    xb = [c1.f32(D) for _ in range(NXB)]
    sqb = c1.f32(D)
    hnb = [c1.bf(D) for _ in range(NXB)]
    ssb = c1.f32(2 * NXB)
    P.dma('sp', nwbc, bcv[0], w=['nwbc'], sem='c1')
    for tt in range(NT2):
        s = tt % NXB
        xs, hs = xb[s], hnb[s]
        P.dma('sp', xs, xl[tt * 128:(tt + 1) * 128, :], w=['xb%d' % s], sem='x%d' % s)
        P.act(lambda e, xs=xs: e.activation(out=sqb, in_=xs, func=AF.Square), r=['xb%d' % s], w=['sqb'])
        ss1 = ssb[:, s:s + 1]
        rs1 = ssb[:, NXB + s:NXB + s + 1]
        P.dve(lambda e, ss1=ss1: e.tensor_reduce(out=ss1, in_=sqb, axis=AX.X, op=ALU.add), r=['sqb'], w=['ss%d' % s])
        rstd_from_ss(ss1, rs1, D, ['ss%d' % s], ['rs%d' % s], small[:, 60:61])
        P.dve(lambda e, xs=xs, hs=hs, rs1=rs1: e.scalar_tensor_tensor(out=hs, in0=xs, scalar=rs1, in1=nwbc,
                                                                     op0=ALU.mult, op1=ALU.mult),
              r=['xb%d' % s, 'rs%d' % s, 'nwbc'], w=['hnb%d' % s])
        for g in range((KC + 7) // 8):
            bank = (2 * s + (g % 2)) % 8
            n = min(8, KC - 8 * g)
            for j in range(n):
                kc = 8 * g + j
                P.pe(lambda e, bank=bank, j=j, kc=kc, hs=hs: e.transpose(
                    out=psb[bank][:, j * 128:(j + 1) * 128], in_=hs[:, kc * 128:(kc + 1) * 128], identity=ident),
                    r=['hnb%d' % s, 'ident'], w=['ps%d' % bank])
            a = tt * 128
            base = hnT_own if a < T else regB
            aa = a if a < T else a - T
            dst = base.rearrange("p (k t) -> p k t", t=T)[:, 8 * g:8 * g + n, aa:aa + 128]
            src = psb[bank][:, 0:n * 128].rearrange("p (k t) -> p k t", t=128)
            eng = P.act if g % 2 == 0 else P.dve
            if g % 2 == 0:
                P.act(lambda e, dst=dst, src=src: e.activation(out=dst, in_=src, func=AF.Copy),
                      r=['ps%d' % bank], w=[hn_res(a)])
            else:
                P.dve(lambda e, dst=dst, src=src: e.tensor_copy(out=dst, in_=src),
                      r=['ps%d' % bank], w=[hn_res(a)])
    P.dve(lambda e: e.tensor_copy(out=halo.rearrange("p (k c) -> p k c", c=2),
                                  in_=regB.rearrange("p (k t) -> p k t", t=T)[:, :, 0:2]),
          r=['hnT_par'], w=['halo'])
    if cfg.debug:
        P.dma('pool', dbg['d_hnT'][:, 0:KC * T], hnT_own, r=['hnT_own'], w=['dbg0'], sem='dbg')
        P.dma('pool', dbg['d_hnT'][:, KC * T:2 * KC * T], regB, r=['hnT_par'], w=['dbg0'], sem='dbg')
    P.barrier()

    def proj_fm(bank, wslot, wres, col0, tok0, ncols=512):
        gi_ = col0 // 128
        for kc in range(KC):
            lh = wslot[:, (gi_ * KC + kc) * 128:(gi_ * KC + kc + 1) * 128]
            rh = hn_any(kc, tok0, tok0 + ncols)
            P.pe(lambda e, kc=kc, lh=lh, rh=rh: e.matmul(ps[bank][:, 0:ncols], lhsT=lh, rhs=rh,
                                                         start=(kc == 0), stop=(kc == KC - 1)),
                 r=[wres, hn_res(tok0)], w=['ps%d' % bank])

    wslot_w = [0]

    c2 = Carve(phase_base)
    wslot_w[0] = 256
    wB = [c2.bf(KC * 256) for _ in range(2)]
    b_t2 = [c2.f32(512) for _ in range(2)]
    b_lfp = [c2.f32(1 + T) for _ in range(2)]
    b_kk = [c2.f32(T) for _ in range(2)]
    b_e = [c2.f32(T) for _ in range(2)]
    b_ones = c2.f32(T)
    b_kw = [c2.bf(T) for _ in range(2)]
    b_kwtok = [c2.bf(T) for _ in range(2)]
    b_vtok = [c2.bf(T) for _ in range(2)]
    for q_ in range(2):
        P.dve(lambda e, q_=q_: e.memset(b_lfp[q_][:, 0:1], 0.0), w=['b_lfp%d' % q_])
    P.dve(lambda e: e.memset(b_ones, 1.0), w=['b_ones'])

    def b0_load(h):
        s = h % 2
        for gi, g in enumerate((1, 3)):
            wdma(wB[s][:, gi * KC * 128:(gi + 1) * KC * 128], wu_in[g * NH + h], r=[], w=['wB%d' % s], key='wB%d' % s)

    def b0_head(h):
        s = h % 2
        w_ = wB[s]
        wres = 'wB%d' % s
        lb1 = lbv[:, NH + h:NH + h + 1]
        om1 = omlv[:, NH + h:NH + h + 1]
        t2, lfp, kk_, e_, kw, kwtok, vtk = b_t2[s], b_lfp[s], b_kk[s], b_e[s], b_kw[s], b_kwtok[s], b_vtok[s]
        for tb in range(NB):
            bank = tb % 2
            proj_fm(bank, w_, wres, 128, T + tb * 512)
            tq = b_t2[tb % 2] if False else t2
            act_sigmoid(t2, ps[bank], ['ps%d' % bank], ['b_t2%d' % s])
            P.dve(lambda e: e.tensor_scalar(out=t2, in0=t2, scalar1=om1, scalar2=lb1, op0=ALU.mult, op1=ALU.add),
                  r=['b_t2%d' % s, 'omlv', 'lbv'], w=['b_t2%d' % s])
            P.act(lambda e, tb=tb: e.activation(out=lfp[:, 1 + tb * 512:1 + (tb + 1) * 512], in_=t2, func=AF.Ln),
                  r=['b_t2%d' % s], w=['b_lfp%d' % s])
            P.pool(lambda e, tb=tb: e.tensor_scalar(out=kk_[:, tb * 512:(tb + 1) * 512], in0=t2, scalar1=-1.0, scalar2=1.0,
                                                    op0=ALU.mult, op1=ALU.add), r=['b_t2%d' % s], w=['b_kk%d' % s])
        for g in range((NT + 3) // 4):
            n = min(4, NT - 4 * g)
            vb = 4 + (g % 2)
            for j in range(n):
                tt = 4 * g + j
                for kc in range(KC):
                    P.pe(lambda e, j=j, tt=tt, kc=kc, vb=vb: e.matmul(ps[vb][:, j * 128:(j + 1) * 128],
                                                                      lhsT=hn_par(kc, tt * 128, (tt + 1) * 128),
                                                                      rhs=w_[:, kc * 128:(kc + 1) * 128],
                                                                      start=(kc == 0), stop=(kc == KC - 1)),
                         r=[wres, 'hnT_par'], w=['ps%d' % vb])
            P.act(lambda e, g=g, n=n, vb=vb: e.activation(out=vtk[:, g * 512:g * 512 + n * 128], in_=ps[vb][:, 0:n * 128],
                                                          func=AF.Copy), r=['ps%d' % vb], w=['b_vtok%d' % s])
        P.dve(lambda e: e.tensor_tensor_scan(out=e_, data0=lfp[:, 0:T], data1=b_ones, initial=0.0,
                                             op0=ALU.add, op1=ALU.mult), r=['b_lfp%d' % s, 'b_ones'], w=['b_e%d' % s])
        P.act(lambda e: e.activation(out=e_, in_=e_, func=AF.Exp), r=['b_e%d' % s], w=['b_e%d' % s])
        P.dve(lambda e: e.tensor_tensor(out=kw, in0=kk_, in1=e_, op=ALU.mult), r=['b_kk%d' % s, 'b_e%d' % s], w=['b_kw%d' % s])
        tbk = 2 + s
        for g in range(NT // 8 if NT >= 8 else 1):
            n = min(8, NT - 8 * g)
            for j in range(n):
                tt = 8 * g + j
                P.pe(lambda e, j=j, tt=tt: e.transpose(out=psb[tbk][:, j * 128:(j + 1) * 128],
                                                       in_=kw[:, tt * 128:(tt + 1) * 128], identity=ident),
                     r=['b_kw%d' % s, 'ident'], w=['ps%d' % tbk])
            P.dve(lambda e, g=g, n=n: e.tensor_copy(out=kwtok[:, g * 1024:g * 1024 + n * 128], in_=psb[tbk][:, 0:n * 128]),
                  r=['ps%d' % tbk], w=['b_kwtok%d' % s])
        sb = 6 + s
        for tt in range(NT):
            P.pe(lambda e, tt=tt: e.matmul(ps[sb][:, 0:128], lhsT=kwtok[:, tt * 128:(tt + 1) * 128],
                                           rhs=vtk[:, tt * 128:(tt + 1) * 128], start=(tt == 0), stop=(tt == NT - 1)),
                 r=['b_kwtok%d' % s, 'b_vtok%d' % s], w=['ps%d' % sb])
        P.dve(lambda e: e.tensor_copy(out=sinit[:, h * 128:(h + 1) * 128], in_=ps[sb][:, 0:128]), r=['ps%d' % sb], w=['sinit'])

    b0_load(0)
    if NH > 1:
        b0_load(1)
    for h in range(NH):
        b0_head(h)
        if h + 2 < NH:
            b0_load(h + 2)
    if cfg.debug:
        P.dma('sp', dbg['d_sinit'], sinit, r=['sinit'], w=['dbg1'], sem='dbg1')
    P.barrier()

    c3 = Carve(phase_base)
    wslot_w[0] = 640
    wA = [c3.bf(KC * 640) for _ in range(2)]
    cB = Carve(phase_base)
    so_words = 0

    class RB:
        off = 0

    def rb_bf(n):
        if 12 * T > KC * T:
            return c3.bf(n)
        o = RB.off
        RB.off += n
        assert RB.off <= KC * T
        return regB[:, o:o + n]

    qrel = [[rb_bf(T) for _ in range(2)] for _ in range(2)]
    krelT = [[rb_bf(T) for _ in range(2)] for _ in range(2)]
    vtok = [rb_bf(T) for _ in range(2)]
    zs = [rb_bf(T) for _ in range(2)]
    NCK = T // CH
    scal = [[c3.f32(3 * NCK) for _ in range(2)] for _ in range(2)]
    a_qs = c3.f32(512)
    a_qs2 = [a_qs, cstage]
    rbrem = regB.bitcast(F32)
    if 12 * T <= KC * T and (KC * T - 12 * T) // 2 >= 4 * 512:
        _o = 12 * T // 2
        xtra = [rbrem[:, _o + i * 512:_o + (i + 1) * 512] for i in range(4)]
    else:
        xtra = [c3.f32(512) for _ in range(4)]
    a_t2 = [c3.f32(512), xtra[0]]
    a_t3 = [c3.f32(512), xtra[1]]
    a_lfp = [c3.f32(513), c3.f32(513)]
    a_c = [c3.f32(512), xtra[2]]
    a_a = [c3.f32(512), xtra[3]]
    a_qm = c3.f32(512)
    a_km = c3.f32(512)
    a_tt = c3.f32(2 * 8)
    g_ktok = [c3.bf(T) for _ in range(2)]
    g_ktokh = [c3.bf(T) for _ in range(2)]
    g_scm = [c3.bf(T) for _ in range(2)]
    KVR = min(8, NCK)
    g_kvs = [c3.f32(KVR * 128) for _ in range(2)]
    g_sbf = [c3.bf(NCK * 128) for _ in range(2)]
    g_S4 = [c3.f32(128) for _ in range(4)]
    g_osq = c3.f32(512)
    g_rs = c3.f32(512)
    g_on = g_osq
    g_ln = g_rs
    for q_ in range(2):
        P.dve(lambda e, q_=q_: e.memset(a_lfp[q_][:, 0:1], 0.0), w=['a_lfp%d' % q_])

    def a_load(h):
        s = h % 2
        for g in range(5):
            wdma(wA[s][:, g * KC * 128:(g + 1) * KC * 128], wu_in[g * NH + h], r=[], w=['wA%d' % s], key='wA%d_%d' % (s, g % 2))

    def a_proj(h):
        s = h % 2
        w_ = wA[s]
        wres = 'wA%d' % s
        pend = [None]
        for tb in range(NB):
            t0 = tb * 512
            nck = 512 // CH
            qs_ = a_qs2[tb % 2]
            qsr = 'a_qs%d' % (tb % 2)
            proj_fm(0, w_, wres, 0, t0)
            act_sigmoid(qs_, ps[0], ['ps0'], [qsr])
            P.dve(lambda e, qs_=qs_: e.tensor_tensor(out=qs_, in0=ps[0], in1=qs_, op=ALU.mult), r=['ps0', qsr], w=[qsr])
            proj_fm(1, w_, wres, 4 * 128, t0)
            act_sigmoid(a_km, ps[1], ['ps1'], ['a_km'])
            P.dve(lambda e, t0=t0: e.tensor_tensor(out=zs[s][:, t0:t0 + 512], in0=ps[1], in1=a_km, op=ALU.mult),
                  r=['ps1', 'a_km'], w=['zs%d' % s])
            yield
            def it_body(d, tb=tb, t0=t0, nck=nck, qs_=qs_, qsr=qsr):
                bank = d % 2
                lb1 = lbv[:, d * NH + h:d * NH + h + 1]
                om1 = omlv[:, d * NH + h:d * NH + h + 1]
                proj_fm(bank, w_, wres, (2 + d) * 128, t0)
                nom1 = nomlv[:, d * NH + h:d * NH + h + 1]
                t2_, t3_, lfp_, c_, aa_ = a_t2[d], a_t3[d], a_lfp[d], a_c[d], a_a[d]
                act_sigmoid(t2_, ps[bank], ['ps%d' % bank], ['a_t2%d' % d])
                P.act(lambda e, t2_=t2_, lfp_=lfp_, lb1=lb1, om1=om1: e.activation(out=lfp_[:, 1:513], in_=t2_, func=AF.Ln,
                                                                                   bias=lb1, scale=om1),
                      r=['a_t2%d' % d, 'lbv', 'omlv'], w=['a_lfp%d' % d])
                P.pool(lambda e, t2_=t2_, t3_=t3_, nom1=nom1, om1=om1: e.tensor_scalar(out=t3_, in0=t2_, scalar1=nom1, scalar2=om1,
                                                                                     op0=ALU.mult, op1=ALU.add),
                       r=['a_t2%d' % d, 'omlv', 'nomlv'], w=['a_t3%d' % d])
                if d == 0:
                    P.dve(lambda e, c_=c_, lfp_=lfp_: e.tensor_tensor_scan(out=c_, data0=rmask, data1=lfp_[:, 1:513], initial=0.0,
                                                                         op0=ALU.mult, op1=ALU.add),
                          r=['rmask', 'a_lfp%d' % d], w=['a_c%d' % d])
                    ref = 31
                else:
                    P.dve(lambda e, c_=c_, lfp_=lfp_: e.tensor_tensor_scan(out=c_, data0=lfp_[:, 0:512], data1=rmask, initial=0.0,
                                                                         op0=ALU.add, op1=ALU.mult),
                          r=['rmask', 'a_lfp%d' % d], w=['a_c%d' % d])
                    ref = 32
                c3v = c_.rearrange("p (n c) -> p n c", c=CH)
                a3v = aa_.rearrange("p (n c) -> p n c", c=CH)
                P.dve(lambda e, c3v=c3v, a3v=a3v, ref=ref: e.tensor_tensor(
                    out=a3v, in0=c3v, in1=c3v[:, :, ref:ref + 1].to_broadcast([128, nck, CH]), op=ALU.subtract),
                    r=['a_c%d' % d], w=['a_a%d' % d])
                yield
                sc_ = scal[s][d]
                dec = sc_[:, 0 * NCK + tb * nck:0 * NCK + (tb + 1) * nck]
                gg = sc_[:, 1 * NCK + tb * nck:1 * NCK + (tb + 1) * nck]
                scc = sc_[:, 2 * NCK + tb * nck:2 * NCK + (tb + 1) * nck]
                sres = 'scal%d_%d' % (s, d)
                c_last = c3v[:, :, CH - 1:CH].rearrange("p n o -> p (n o)")
                c_ref = c3v[:, :, ref:ref + 1].rearrange("p n o -> p (n o)")
                a_last = a3v[:, :, CH - 1:CH].rearrange("p n o -> p (n o)")
                if d == 0:
                    P.act(lambda e, dec=dec, c_last=c_last: e.activation(out=dec, in_=c_last, func=AF.Exp),
                          r=['a_c%d' % d], w=[sres])
                    P.act(lambda e, scc=scc, c_ref=c_ref: e.activation(out=scc, in_=c_ref, func=AF.Exp),
                          r=['a_c%d' % d], w=[sres])
                    P.act(lambda e, gg=gg, a_last=a_last: e.activation(out=gg, in_=a_last, func=AF.Exp),
                          r=['a_a%d' % d], w=[sres])
                else:
                    lf_last = lfp_[:, 1:513].rearrange("p (n c) -> p n c", c=CH)[:, :, CH - 1:CH].rearrange("p n o -> p (n o)")
                    tt_ = a_tt[:, 0:nck]
                    tt2 = a_tt[:, 8:8 + nck]
                    P.dve(lambda e, tt_=tt_, c_last=c_last, lf_last=lf_last: e.tensor_tensor(
                        out=tt_, in0=c_last, in1=lf_last, op=ALU.add), r=['a_c%d' % d, 'a_lfp%d' % d], w=['a_tt'])
                    P.dve(lambda e, tt_=tt_, tt2=tt2, c_ref=c_ref: e.tensor_tensor(
                        out=tt2, in0=tt_, in1=c_ref, op=ALU.subtract), r=['a_c%d' % d, 'a_tt'], w=['a_tt2'])
                    P.act(lambda e, dec=dec, tt_=tt_: e.activation(out=dec, in_=tt_, func=AF.Exp), r=['a_tt'], w=[sres])
                    P.act(lambda e, gg=gg, c_ref=c_ref: e.activation(out=gg, in_=c_ref, func=AF.Exp), r=['a_c%d' % d], w=[sres])
                    P.act(lambda e, scc=scc, tt2=tt2: e.activation(out=scc, in_=tt2, func=AF.Exp), r=['a_tt2'], w=[sres])
                sq = 1.0 if d == 0 else -1.0
                P.act(lambda e, sq=sq, aa_=aa_: e.activation(out=a_qm, in_=aa_, func=AF.Exp, scale=sq), r=['a_a%d' % d], w=['a_qm'])
                P.act(lambda e, sq=sq, aa_=aa_: e.activation(out=a_km, in_=aa_, func=AF.Exp, scale=-sq), r=['a_a%d' % d], w=['a_km'])
                P.dve(lambda e, d=d, t0=t0: e.tensor_tensor(out=qrel[s][d][:, t0:t0 + 512], in0=qs_, in1=a_qm, op=ALU.mult),
                      r=[qsr, 'a_qm'], w=['qrel%d_%d' % (s, d)])
                P.pool(lambda e, d=d, t0=t0, t3_=t3_: e.tensor_tensor(out=krelT[s][d][:, t0:t0 + 512], in0=t3_, in1=a_km, op=ALU.mult),
                       r=['a_t3%d' % d, 'a_km'], w=['krelT%d_%d' % (s, d)])
            for d in range(2):
                g_ = it_body(d)
                next(g_)
                if pend[0] is not None:
                    for _ in pend[0]:
                        pass
                pend[0] = g_
                yield
            for j in range(4):
                tt = tb * 4 + j
                for kc in range(KC):
                    P.pe(lambda e, j=j, tt=tt, kc=kc: e.matmul(ps[2][:, j * 128:(j + 1) * 128],
                                                               lhsT=hn_own(kc, tt * 128, (tt + 1) * 128),
                                                               rhs=w_[:, (KC + kc) * 128:(KC + kc + 1) * 128],
                                                               start=(kc == 0), stop=(kc == KC - 1)),
                         r=[wres, 'hnT_own'], w=['ps2'])
            P.dve(lambda e, t0=t0: e.tensor_copy(out=vtok[s][:, t0:t0 + 512], in_=ps[2]), r=['ps2'], w=['vtok%d' % s])
            yield
        if pend[0] is not None:
            for _ in pend[0]:
                pass
        yield


    def a_gla(h):
        s = h % 2
        masks = (mask_o, mask_i)
        for d in range(2):
            kT = krelT[s][d]
            qr = qrel[s][d]
            sres = 'scal%d_%d' % (s, d)
            sc_ = scal[s][d]
            if d == 0:
                P.dve(lambda e: e.memset(g_S4[0], 0.0), w=['g_S0'])
            else:
                P.dve(lambda e: e.tensor_copy(out=g_S4[0], in_=sinit[:, h * 128:(h + 1) * 128]), r=['sinit'], w=['g_S0'])
            si = [0]
            gorder = list(range(NT // 4)) if d == 0 else list(range(NT // 4 - 1, -1, -1))
            for g in gorder:
                for j in range(4):
                    tt = 4 * g + j
                    P.pe(lambda e, j=j, tt=tt, kT=kT: e.transpose(out=psb[3][:, j * 128:(j + 1) * 128],
                                                                  in_=kT[:, tt * 128:(tt + 1) * 128], identity=ident),
                         r=['krelT%d_%d' % (s, d), 'ident'], w=['ps3'])
                P.dve(lambda e, g=g, d=d: e.tensor_scalar(out=g_ktok[d][:, g * 512:(g + 1) * 512], in0=psb[3][:, 0:512],
                                                          scalar1=mcol[:, 0:1], scalar2=None, op0=ALU.mult),
                      r=['ps3', 'mcol'], w=['g_ktok%d' % d])
                P.dve(lambda e, g=g, d=d: e.tensor_scalar(out=g_ktokh[d][:, g * 512:(g + 1) * 512], in0=psb[3][:, 0:512],
                                                          scalar1=mcol[:, 1:2], scalar2=None, op0=ALU.mult),
                      r=['ps3', 'mcol'], w=['g_ktok%d' % d])
                yield
                for j in range(4):
                    tt = 4 * g + j
                    P.pe(lambda e, j=j, tt=tt, kT=kT, qr=qr: e.matmul(ps[4][:, j * 128:(j + 1) * 128],
                                                                      lhsT=kT[:, tt * 128:(tt + 1) * 128],
                                                                      rhs=qr[:, tt * 128:(tt + 1) * 128], start=True, stop=True),
                         r=['krelT%d_%d' % (s, d), 'qrel%d_%d' % (s, d)], w=['ps4'])
                P.dve(lambda e, g=g, d=d: e.tensor_tensor(out=g_scm[d][:, g * 512:(g + 1) * 512], in0=ps[4], in1=masks[d],
                                                          op=ALU.mult), r=['ps4', 'mask_o', 'mask_i'], w=['g_scm%d' % d])
                yield
                for half in range(2):
                    for jj in range(4):
                        n = 8 * g + 4 * half + jj
                        tt = n // 2
                        r0 = (n % 2) * 64
                        P.pe(lambda e, jj=jj, tt=tt, r0=r0, d=d: e.matmul(
                            ps[5][:, jj * 128:(jj + 1) * 128],
                            lhsT=(g_ktok[d] if r0 == 0 else g_ktokh[d])[:, tt * 128:(tt + 1) * 128],
                            rhs=vtok[s][:, tt * 128:(tt + 1) * 128], start=True, stop=True),
                            r=['g_ktok%d' % d, 'vtok%d' % s], w=['ps5'])
                    if half == 1:
                        yield
                    n0 = 8 * g + 4 * half
                    sl0 = (n0 % KVR) * 128
                    gsl = sc_[:, NCK + n0:NCK + n0 + 4]
                    P.dve(lambda e, d=d, sl0=sl0, gsl=gsl: e.tensor_tensor(
                        out=g_kvs[d][:, sl0:sl0 + 512].rearrange("p (n c) -> p n c", c=128),
                        in0=ps[5].rearrange("p (n c) -> p n c", c=128),
                        in1=gsl.unsqueeze(2).to_broadcast([128, 4, 128]), op=ALU.mult),
                        r=['ps5', sres], w=['g_kvs%d_%d' % (d, (n0 + q_) % KVR) for q_ in range(4)])
                yield
                order = range(8 * g, 8 * g + 8) if d == 0 else range(8 * g + 7, 8 * g - 1, -1)
                for n in order:
                    cur = si[0] % 4
                    nx = (si[0] + 1) % 4
                    si[0] += 1
                    P.act(lambda e, n=n, d=d, sc_=sc_, cur=cur: e.activation(
                        out=g_sbf[d][:, n * 128:(n + 1) * 128], in_=g_S4[cur], func=AF.Identity,
                        scale=sc_[:, 2 * NCK + n:2 * NCK + n + 1]), r=['g_S%d' % cur, sres], w=['g_sbf%d' % d])
                    P.dve(lambda e, n=n, d=d, sc_=sc_, cur=cur, nx=nx: e.scalar_tensor_tensor(
                        out=g_S4[nx], in0=g_S4[cur], scalar=sc_[:, n:n + 1], in1=g_kvs[d][:, (n % KVR) * 128:(n % KVR + 1) * 128],
                        op0=ALU.mult, op1=ALU.add), r=['g_S%d' % cur, 'g_kvs%d_%d' % (d, n % KVR), sres], w=['g_S%d' % nx])
                    if n % 2 == 1:
                        yield
                yield
        for g in range(NT // 4):
            ob = 6 if g % 2 == 0 else 3
            for j in range(4):
                tt = 4 * g + j
                mm = []
                for d in range(2):
                    mm.append((vtok[s][:, tt * 128:(tt + 1) * 128], g_scm[d][:, tt * 128:(tt + 1) * 128],
                               slice(j * 128, (j + 1) * 128)))
                for c in range(2):
                    n = 2 * tt + c
                    for d in range(2):
                        mm.append((g_sbf[d][:, n * 128:(n + 1) * 128], qrel[s][d][:, n * CH:(n + 1) * CH],
                                   slice(j * 128 + c * CH, j * 128 + (c + 1) * CH)))
                for i_, (l_, r_, sl_) in enumerate(mm):
                    P.pe(lambda e, l_=l_, r_=r_, sl_=sl_, i_=i_, ob=ob, last=(i_ == len(mm) - 1): e.matmul(
                        ps[ob][:, sl_], lhsT=l_, rhs=r_, start=(i_ == 0), stop=last),
                        r=['vtok%d' % s, 'g_scm0', 'g_scm1', 'g_sbf0', 'g_sbf1', 'qrel%d_0' % s, 'qrel%d_1' % s], w=['ps%d' % ob])
            P.act(lambda e, ob=ob: e.activation(out=g_osq, in_=ps[ob], func=AF.Square), r=['ps%d' % ob], w=['g_osq'])
            P.pe(lambda e: e.matmul(ps[7], lhsT=ones_f[:, 0:128], rhs=g_osq, start=True, stop=True),
                 r=['ones_f', 'g_osq'], w=['ps7'])
            P.act(lambda e: e.activation(out=g_ln, in_=ps[7], func=AF.Ln, bias=epsv, scale=1.0 / 128), r=['ps7', 'epsv'], w=['g_rs'])
            P.act(lambda e: e.activation(out=g_rs, in_=g_ln, func=AF.Exp, scale=-0.5), r=['g_rs'], w=['g_rs'])
            P.dve(lambda e, ob=ob: e.tensor_tensor(out=g_on, in0=ps[ob], in1=g_rs, op=ALU.mult), r=['ps%d' % ob, 'g_rs'], w=['g_osq'])
            P.dve(lambda e, g=g: e.scalar_tensor_tensor(out=omixA[:, h * T + g * 512:h * T + (g + 1) * 512], in0=g_on,
                                                        scalar=chcol(4, h), in1=zs[s][:, g * 512:(g + 1) * 512],
                                                        op0=ALU.mult, op1=ALU.mult),
                  r=['g_osq', 'chv', 'zs%d' % s], w=['omixA'])
            yield

    a_load(0)
    if NH > 1:
        a_load(1)
    for h in range(NH + 1):
        gens = []
        if h < NH:
            gens.append(a_proj(h))
        if h >= 1:
            gens.append(a_gla(h - 1))
        interleave(gens)
        if h + 2 < NH:
            a_load(h + 2)
    P.barrier()

    c4 = Carve(phase_base)
    wslot_w[0] = 512
    wC = [c4.bf(KC * 512) for _ in range(2)]
    omixB = regB
    c_cs = c4.f32(512)
    c_u = c4.f32(T + 2)
    c_y = c4.f32(512)
    c_yb = c4.f32(512)
    c_sq = c4.f32(512)
    c_z = c4.f32(512)
    c_h = c4.f32(2)
    c_ss = c4.f32(NT)
    P.dve(lambda e: e.memset(c_ss, 0.0), w=['c_ss'])
    P.dve(lambda e: e.memset(c_u[:, 0:1], 0.0), w=['c_u'])

    def c_load(cb):
        s = cb % 2
        for g in range(4):
            wdma(wC[s][:, g * KC * 128:(g + 1) * KC * 128], wu_in[(5 + g) * NH + cb], r=[], w=['wC%d' % s], key='wC%d_%d' % (s, g % 2))

    def c_block(cb):
        s = cb % 2
        w_ = wC[s]
        wres = 'wC%d' % s
        for tb in range(NB):
            t0 = tb * 512
            proj_fm(0, w_, wres, 1 * 128, t0)
            P.act(lambda e: e.activation(out=c_cs, in_=ps[0], func=AF.Copy), r=['ps0'], w=['c_cs'])
            proj_fm(1, w_, wres, 2 * 128, t0)
            P.dve(lambda e, t0=t0: e.tensor_tensor(out=c_u[:, 1 + t0:1 + t0 + 512], in0=c_cs, in1=ps[1], op=ALU.mult),
                  r=['c_cs', 'ps1'], w=['c_u'])
        for gi, g in enumerate((1, 2)):
            for kc in range(KC):
                P.pe(lambda e, gi=gi, g=g, kc=kc: e.matmul(ps[2][:, gi * 2:gi * 2 + 2], lhsT=w_[:, (g * KC + kc) * 128:(g * KC + kc + 1) * 128],
                                                           rhs=halo[:, kc * 2:kc * 2 + 2], start=(kc == 0), stop=(kc == KC - 1)),
                     r=[wres, 'halo'], w=['ps2'])
        P.act(lambda e: e.activation(out=c_h, in_=ps[2][:, 0:2], func=AF.Copy), r=['ps2'], w=['c_h'])
        P.dve(lambda e: e.tensor_tensor(out=c_u[:, T + 1:T + 2], in0=c_h[:, 0:1], in1=ps[2][:, 2:3], op=ALU.mult),
              r=['c_h', 'ps2'], w=['c_u'])
        for tb in range(NB):
            t0 = tb * 512
            P.dve(lambda e, t0=t0: e.tensor_scalar(out=c_y, in0=c_u[:, t0:t0 + 512], scalar1=chcol(5, cb), scalar2=None,
                                                   op0=ALU.mult), r=['c_u', 'chv'], w=['c_y'])
            P.dve(lambda e, t0=t0: e.scalar_tensor_tensor(out=c_y, in0=c_u[:, t0 + 1:t0 + 513], scalar=chcol(6, cb), in1=c_y,
                                                          op0=ALU.mult, op1=ALU.add), r=['c_u', 'chv', 'c_y'], w=['c_y'])
            P.dve(lambda e, t0=t0: e.scalar_tensor_tensor(out=c_y, in0=c_u[:, t0 + 2:t0 + 514], scalar=chcol(7, cb), in1=c_y,
                                                          op0=ALU.mult, op1=ALU.add), r=['c_u', 'chv', 'c_y'], w=['c_y'])
            proj_fm(3, w_, wres, 0, t0)
            P.dve(lambda e: e.tensor_tensor(out=c_yb, in0=c_y, in1=ps[3], op=ALU.mult), r=['c_y', 'ps3'], w=['c_yb'])
            P.act(lambda e: e.activation(out=c_sq, in_=c_yb, func=AF.Square), r=['c_yb'], w=['c_sq'])
            for j in range(4):
                tt = tb * 4 + j
                P.pe(lambda e, j=j, tt=tt: e.matmul(ps[7][:, tt:tt + 1], lhsT=c_sq[:, j * 128:(j + 1) * 128],
                                                    rhs=ones_f[:, 0:1], start=True, stop=True),
                     r=['c_sq', 'ones_f'], w=['ps7'])
            P.dve(lambda e, tb=tb: e.tensor_tensor(out=c_ss[:, tb * 4:tb * 4 + 4], in0=c_ss[:, tb * 4:tb * 4 + 4],
                                                   in1=ps[7][:, tb * 4:tb * 4 + 4], op=ALU.add), r=['ps7', 'c_ss'], w=['c_ss'])
            proj_fm(4, w_, wres, 3 * 128, t0)
            act_sigmoid(c_z, ps[4], ['ps4'], ['c_z'])
            P.dve(lambda e: e.tensor_tensor(out=c_z, in0=ps[4], in1=c_z, op=ALU.mult), r=['ps4', 'c_z'], w=['c_z'])
            P.dve(lambda e, t0=t0: e.scalar_tensor_tensor(out=omixB[:, cb * T + t0:cb * T + t0 + 512], in0=c_yb,
                                                          scalar=chcol(8, cb), in1=c_z, op0=ALU.mult, op1=ALU.mult),
                  r=['c_yb', 'chv', 'c_z'], w=['omixB'])

    c_load(0)
    if NH > 1:
        c_load(1)
    for cb in range(NH):
        c_block(cb)
        if cb + 2 < NH:
            c_load(cb + 2)
    rstd_from_ss(c_ss, rstdB, D, ['c_ss'], ['rstdB'], small[:, 32:32 + NT])
    if cfg.debug:
        P.dma('pool', dbg['d_omix'][:, 0:KC * T], omixA, r=['omixA'], w=['dbg2'], sem='dbg')
        P.dma('pool', dbg['d_omix'][:, KC * T:2 * KC * T], omixB, r=['omixB'], w=['dbg2'], sem='dbg')
    P.barrier()

    c5 = Carve(phase4_base)
    h2 = c5.f32(NT * D)
    wO = [c5.bf(KC * 512) for _ in range(3)]
    for tt in range(NT):
        P.dma('sp', h2[:, tt * D:(tt + 1) * D], xl[tt * 128:(tt + 1) * 128, :], w=['h2_%d' % tt], sem='xh%d' % (tt % 4))
    ounits = [(j, part) for j in range(NGB) for part in range(2)]

    def o_load(u):
        s = u % 3
        half = (KC // 2) * 512 if KC >= 2 else KC * 512
        wdma(wO[s][:, 0:half], wu_out[u][:, 0:half], r=[], w=['wO%d' % s], key='wO%d' % s)
        if half < KC * 512:
            wdma(wO[s][:, half:KC * 512], wu_out[u][:, half:KC * 512], r=[], w=['wO%d' % s], key='wO%d' % s)

    def o_block(j):
        sa = (2 * j) % 3
        sb_ = (2 * j + 1) % 3
        for tt in range(NT):
            ba = 0 + (tt % 2)
            bb = 2 + (tt % 2)
            for kc in range(KC):
                P.pe(lambda e, kc=kc, ba=ba, tt=tt: e.matmul(ps[ba], lhsT=omixA[:, kc * T + tt * 128:kc * T + (tt + 1) * 128],
                                                             rhs=wO[sa][:, kc * 512:(kc + 1) * 512], start=(kc == 0), stop=(kc == KC - 1)),
                     r=['omixA', 'wO%d' % sa], w=['ps%d' % ba])
            for kc in range(KC):
                P.pe(lambda e, kc=kc, bb=bb, tt=tt: e.matmul(ps[bb], lhsT=omixB[:, kc * T + tt * 128:kc * T + (tt + 1) * 128],
                                                             rhs=wO[sb_][:, kc * 512:(kc + 1) * 512], start=(kc == 0), stop=(kc == KC - 1)),
                     r=['omixB', 'wO%d' % sb_], w=['ps%d' % bb])
            hv = h2[:, tt * D + j * 512:tt * D + (j + 1) * 512]
            P.dve(lambda e, hv=hv, ba=ba: e.tensor_tensor(out=hv, in0=hv, in1=ps[ba], op=ALU.add),
                  r=['ps%d' % ba, 'h2_%d' % tt], w=['h2_%d' % tt])
            P.dve(lambda e, hv=hv, bb=bb, tt=tt: e.scalar_tensor_tensor(out=hv, in0=ps[bb], scalar=rstdB[:, tt:tt + 1], in1=hv,
                                                                        op0=ALU.mult, op1=ALU.add),
                  r=['ps%d' % bb, 'h2_%d' % tt, 'rstdB'], w=['h2_%d' % tt])

    nxt_o = [0]

    def o_prefetch(upto):
        while nxt_o[0] < min(upto, len(ounits)):
            o_load(nxt_o[0])
            nxt_o[0] += 1

    o_prefetch(3)
    for j in range(NGB):
        o_block(j)
        o_prefetch((j + 1) * 2 + 3)
    if cfg.debug:
        P.dma('sp', dbg['d_h2'], h2, r=['h2_%d' % tt for tt in range(NT)], w=['dbg3'], sem='dbg1')
    P.barrier()

    c6 = Carve(c5.off - 3 * (KC * 256))
    h2T = omixA
    pT = regB[:, 0:KP * T]
    rbf = regB.bitcast(F32)
    if KP * T // 2 + 2 * D <= KC * T // 2:
        e_ob = [rbf[:, KP * T // 2 + i * D:KP * T // 2 + (i + 1) * D] for i in range(2)]
    else:
        e_ob = [c6.f32(D) for _ in range(2)]
    KH = max(1, KC // 2)
    NKH = KC // KH
    wG = [c6.bf(KH * 512) for _ in range(3)]
    wP = [c6.bf(KP * 512) for _ in range(2)]
    fnbc = c6.f32(D)
    e_hb = [c6.bf(D) for _ in range(2)]
    e_sq = e_hb[0].bitcast(F32) if False else None
    e_sqbase = c6.off - D
    e_sq = big[:, e_sqbase:e_sqbase + D]
    e_pb = c6.f32(PLE)
    e_pbb = c6.bf(PLE)
    e_gs = [c6.f32(512) for _ in range(2)]
    e_t = [c6.f32(512) for _ in range(2)]
    e_ss = c6.f32(4)
    P.dma('sp', fnbc, bcv[1], w=['fnbc'], sem='c1')
    for tt in range(NT):
        s = tt % 2
        P.act(lambda e, tt=tt, s=s: e.activation(out=e_hb[s], in_=h2[:, tt * D:(tt + 1) * D], func=AF.Copy),
              r=['h2_%d' % tt], w=['e_hb%d' % s])
        for g in range((KC + 7) // 8):
            bank = 2 * s + (g % 2)
            n = min(8, KC - 8 * g)
            for j in range(n):
                kc = 8 * g + j
                P.pe(lambda e, bank=bank, j=j, kc=kc, s=s: e.transpose(out=psb[bank][:, j * 128:(j + 1) * 128],
                                                                       in_=e_hb[s][:, kc * 128:(kc + 1) * 128], identity=ident),
                     r=['e_hb%d' % s, 'ident'], w=['ps%d' % bank])
            dst = h2T.rearrange("p (k t) -> p k t", t=T)[:, 8 * g:8 * g + n, tt * 128:(tt + 1) * 128]
            src_ = psb[bank][:, 0:n * 128].rearrange("p (k t) -> p k t", t=128)
            P.dve(lambda e, dst=dst, src_=src_: e.tensor_copy(out=dst, in_=src_), r=['ps%d' % bank], w=['h2T'])
        P.dma('sp', e_pb, pl[tt * 128:(tt + 1) * 128, :], w=['e_pb'], sem='pp')
        P.act(lambda e: e.activation(out=e_pbb, in_=e_pb, func=AF.Copy), r=['e_pb'], w=['e_pbb'])
        for kc in range(KP):
            P.pe(lambda e, kc=kc: e.transpose(out=psb[4][:, kc * 128:(kc + 1) * 128], in_=e_pbb[:, kc * 128:(kc + 1) * 128],
                                              identity=ident), r=['e_pbb', 'ident'], w=['ps4'])
        dst = pT.rearrange("p (k t) -> p k t", t=T)[:, 0:KP, tt * 128:(tt + 1) * 128]
        src_ = psb[4][:, 0:KP * 128].rearrange("p (k t) -> p k t", t=128)
        P.dve(lambda e, dst=dst, src_=src_: e.tensor_copy(out=dst, in_=src_), r=['ps4'], w=['pT'])

    gunits = [(j, kh) for j in range(NGB) for kh in range(NKH)]

    def g_load(u):
        j, kh = gunits[u]
        s = u % 3
        wdma(wG[s], wu_g[j][:, kh * KH * 512:(kh + 1) * KH * 512], r=[], w=['wG%d' % s], key='wG%d' % s)
        if kh == 0:
            wdma(wP[j % 2], wu_p[j], r=[], w=['wP%d' % (j % 2)], key='wP%d' % (j % 2))

    def g_block(j):
        for tt in range(NT):
            b1 = 5 + (tt % 2)
            b2 = 0 + (tt % 2)
            for kc in range(KC):
                u = j * NKH + kc // KH
                s = u % 3
                kk_ = kc % KH
                P.pe(lambda e, kc=kc, b1=b1, tt=tt, s=s, kk_=kk_: e.matmul(
                    ps[b1], lhsT=h2T[:, kc * T + tt * 128:kc * T + (tt + 1) * 128],
                    rhs=wG[s][:, kk_ * 512:(kk_ + 1) * 512], start=(kc == 0), stop=(kc == KC - 1)),
                    r=['h2T', 'wG%d' % s], w=['ps%d' % b1])
            sp_ = j % 2
            for kc in range(KP):
                P.pe(lambda e, kc=kc, b2=b2, tt=tt, sp_=sp_: e.matmul(
                    ps[b2], lhsT=pT[:, kc * T + tt * 128:kc * T + (tt + 1) * 128],
                    rhs=wP[sp_][:, kc * 512:(kc + 1) * 512], start=(kc == 0), stop=(kc == KP - 1)),
                    r=['pT', 'wP%d' % sp_], w=['ps%d' % b2])
            q_ = tt % 2
            act_sigmoid(e_gs[q_], ps[b1], ['ps%d' % b1], ['e_gs%d' % q_])
            P.dve(lambda e, b2=b2, q_=q_: e.tensor_tensor(out=e_t[q_], in0=ps[b2], in1=e_gs[q_], op=ALU.mult),
                  r=['ps%d' % b2, 'e_gs%d' % q_], w=['e_t%d' % q_])
            hv = h2[:, tt * D + j * 512:tt * D + (j + 1) * 512]
            P.pool(lambda e, hv=hv, q_=q_: e.tensor_tensor(out=hv, in0=hv, in1=e_t[q_], op=ALU.add),
                   r=['e_t%d' % q_, 'h2_%d' % tt], w=['h2_%d' % tt])

    nxt_u = [0]

    def g_prefetch(upto):
        while nxt_u[0] < min(upto, len(gunits)):
            g_load(nxt_u[0])
            nxt_u[0] += 1

    g_prefetch(3)
    for j in range(NGB):
        g_block(j)
        g_prefetch((j + 1) * NKH + 3)
    for tt in range(NT):
        s = tt % 2
        hv = h2[:, tt * D:(tt + 1) * D]
        P.act(lambda e, hv=hv: e.activation(out=e_sq, in_=hv, func=AF.Square), r=['h2_%d' % tt], w=['e_sq', 'e_hb0', 'e_hb1'])
        ss1 = e_ss[:, s:s + 1]
        rs1 = e_ss[:, 2 + s:3 + s]
        P.dve(lambda e, ss1=ss1: e.tensor_reduce(out=ss1, in_=e_sq, axis=AX.X, op=ALU.add), r=['e_sq'], w=['e_ss%d' % s])
        rstd_from_ss(ss1, rs1, D, ['e_ss%d' % s], ['e_rs%d' % s], small[:, 61:62])
        P.dve(lambda e, hv=hv, rs1=rs1, s=s: e.scalar_tensor_tensor(out=e_ob[s], in0=hv, scalar=rs1, in1=fnbc,
                                                                    op0=ALU.mult, op1=ALU.mult),
              r=['h2_%d' % tt, 'e_rs%d' % s, 'fnbc'], w=['e_ob%d' % s])
        P.dma('sp', y[tt * 128:(tt + 1) * 128, :], e_ob[s], r=['e_ob%d' % s], w=['yout%d' % tt], sem='yo%d' % s)
    P.wait_all('sp', ['yout%d' % tt for tt in range(NT)] + ['dbg0', 'dbg1', 'dbg2', 'dbg3'])

    stop = getattr(cfg, 'stop', None)
    P.emit(nc, es, upto=(P.marks[stop] if stop is not None and stop < len(P.marks) else (stop if stop is not None and stop >= 100 else None)))
    es.close()
    return nc


def _units(w, ncols_unit):
    K, N = w.shape
    kc = K // 128
    nu = N // ncols_unit
    a = w.reshape(kc, 128, nu, ncols_unit).transpose(2, 1, 0, 3)
    return np.ascontiguousarray(a).reshape(nu, 128, kc * ncols_unit)


def _chan(v, nh):
    return np.ascontiguousarray(v.reshape(nh, 128).T)


def _consts():
    c = np.zeros((5, 128, 512), np.float32)
    c[0, :, 0:128] = np.eye(128, dtype=np.float32)
    s = np.arange(128)[:, None]
    t = np.arange(128)[None, :]
    same = (s // CH) == (t // CH)
    mo = (same & (s <= t)).astype(np.float32)
    mi = (same & (s >= t)).astype(np.float32)
    c[1] = np.tile(mo, (1, 4))
    c[2] = np.tile(mi, (1, 4))
    rm = np.ones(512, np.float32)
    rm[::CH] = 0.0
    c[3] = rm[None, :]
    c[4] = 1.0
    return c


def make_in_maps(cfg, x, p, norm_w, w_in, lb_theta, hgrn_norm_w, conv_w, conv_norm_w,
                 w_out, w_ple, w_ple_gate, final_norm_w):
    D, T, NH = cfg.D, cfg.T, cfg.NH
    B = x.shape[0]
    assert x.shape[1] == 2 * T
    f32 = np.float32
    w_in0 = np.asarray(w_in[0], f32)
    groups = [w_in0[:, g * D:(g + 1) * D] for g in range(9)]
    wu = {}
    for par in range(2):
        order = [0, 1, 2, 3, 4, 5, 6, 7, 8] if par == 0 else [0, 1, 3, 2, 4, 5, 6, 7, 8]
        wu[par] = np.concatenate([_units(groups[g], 128) for g in order], axis=0)
    wo = np.asarray(w_out[0], f32)
    uA = _units(wo[:D], 512)
    uB = _units(wo[D:], 512)
    wu_out = np.ascontiguousarray(np.stack([uA, uB], axis=1).reshape(2 * uA.shape[0], 128, -1))
    wu_g = _units(np.asarray(w_ple_gate[0], f32), 512)
    wu_p = _units(np.asarray(w_ple[0], f32), 512)
    bcv = np.stack([np.broadcast_to(np.asarray(norm_w[0], f32), (128, D)),
                    np.broadcast_to(np.asarray(final_norm_w, f32), (128, D))]).astype(f32)
    bcv = np.ascontiguousarray(bcv)
    cst = _consts()
    chv = {}
    for par in range(2):
        do, di = (0, 1) if par == 0 else (1, 0)
        cw = np.asarray(conv_w[0], f32)
        taps = [cw[0], cw[1], cw[2]] if par == 0 else [cw[2], cw[1], cw[0]]
        vecs = [lb_theta[do, 0], lb_theta[do, 1], lb_theta[di, 0], lb_theta[di, 1], hgrn_norm_w[0],
                taps[0], taps[1], taps[2], conv_norm_w[0]]
        chv[par] = np.ascontiguousarray(np.concatenate([_chan(np.asarray(v, f32), NH) for v in vecs], axis=1))
    in_maps = []
    for c in range(2 * B):
        b, par = c // 2, c % 2
        xb = np.asarray(x[b], f32)
        pb = np.asarray(p[0, b], f32)
        if par == 1:
            xb = xb[::-1]
            pb = pb[::-1]
        in_maps.append({
            "xl": np.ascontiguousarray(xb),
            "pl": np.ascontiguousarray(pb[:T]),
            "wu_in": wu[par], "wu_out": wu_out, "wu_g": wu_g, "wu_p": wu_p,
            "chvec": chv[par], "bcv": bcv, "cst": cst,
        })
    return in_maps


def assemble(cfg, results, B):
    T, D = cfg.T, cfg.D
    out = np.empty((B, 2 * T, D), np.float32)
    for c in range(2 * B):
        b, par = c // 2, c % 2
        yc = np.asarray(results[c]["y"], np.float32)
        if par == 0:
            out[b, :T] = yc
        else:
            out[b, T:] = yc[::-1]
    return out


_NC_CACHE = {}


def kernel(x, p, norm_w, w_in, lb_theta, hgrn_norm_w, conv_w, conv_norm_w,
           w_out, w_ple, w_ple_gate, final_norm_w):
    x = np.asarray(x)
    B, S, D = x.shape
    cfg = Cfg(D=D, T=S // 2, PLE=np.asarray(p).shape[-1])
    in_maps = make_in_maps(cfg, x, np.asarray(p), np.asarray(norm_w), np.asarray(w_in), np.asarray(lb_theta),
                           np.asarray(hgrn_norm_w), np.asarray(conv_w), np.asarray(conv_norm_w),
                           np.asarray(w_out), np.asarray(w_ple), np.asarray(w_ple_gate), np.asarray(final_norm_w))
    nc = build(cfg)
    res = run_bass_kernel_spmd(nc, in_maps, core_ids=list(range(2 * B)))
    return assemble(cfg, res.results, B)
```

```python
import numpy as np
from contextlib import ExitStack

import concourse.bass as bass
import concourse.mybir as mybir
from concourse.bass_utils import run_bass_kernel_spmd

F32 = mybir.dt.float32
BF16 = mybir.dt.bfloat16
AF = mybir.ActivationFunctionType
ALU = mybir.AluOpType
AX = mybir.AxisListType

EPS = 1e-6
CH = 64


class Cfg:
    def __init__(self, D=2048, T=1024, PLE=256, debug=False):
        self.D = D
        self.T = T
        self.PLE = PLE
        self.KC = D // 128
        self.NH = D // 128
        self.NT = T // 128
        self.NB = T // 512
        self.debug = debug


class Prog:
    ENG = {'pe': 'tensor', 'act': 'scalar', 'dve': 'vector', 'pool': 'gpsimd', 'sp': 'sync'}

    def __init__(self):
        self.ops = []
        self.lastw = {}
        self.readers = {}
        self.last_eng = {}
        self.last_dma = {}
        self.dma_hist = {}
        self.marks = []
        self.nopfn = {}
        self.fence = None

    def add(self, eng, fn, r=(), w=(), dma=None):
        i = len(self.ops)
        deps = set()
        for x in r:
            lw = self.lastw.get(x)
            if lw is not None:
                deps.add(lw)
        for x in w:
            lw = self.lastw.get(x)
            if lw is not None:
                deps.add(lw)
            deps.update(self.readers.get(x, {}).values())
        rk = dma if dma is not None else eng
        for x in r:
            self.readers.setdefault(x, {})[rk if dma is None else (rk, i)] = i
        for x in w:
            self.lastw[x] = i
            self.readers[x] = {}
        if self.fence is not None:
            deps.add(self.fence)
        deps.discard(i)
        self.ops.append({'eng': eng, 'fn': fn, 'deps': deps, 'dma': dma, 'sig': dma is not None})
        self.last_eng[eng] = i
        if dma is not None:
            self.last_dma[dma] = i
        return i

    def pe(self, fn, r=(), w=()):
        return self.add('pe', fn, r, w)

    def act(self, fn, r=(), w=()):
        return self.add('act', fn, r, w)

    def dve(self, fn, r=(), w=()):
        return self.add('dve', fn, r, w)

    def pool(self, fn, r=(), w=()):
        return self.add('pool', fn, r, w)

    DMA_WIN = 8

    def dma(self, eng, out, in_, r=(), w=(), sem=None):
        lst = self.dma_hist.setdefault(eng, [])
        n = len(lst)
        key = '%s%d' % (eng, n % self.DMA_WIN)
        i = self.add(eng, lambda e: e.dma_start(out=out, in_=in_), r, w, dma=key)
        if n >= self.DMA_WIN:
            self.ops[i]['deps'].add(lst[n - self.DMA_WIN])
        lst.append(i)
        return i

    def barrier(self):
        self.marks.append(len(self.ops))
        targets = set(self.last_eng.values()) | set(self.last_dma.values())
        i = len(self.ops)
        self.ops.append({'eng': 'dve', 'fn': self.nopfn['dve'], 'deps': set(targets), 'dma': None, 'sig': False})
        self.last_eng['dve'] = i
        self.fence = i

    def wait_all(self, eng, res):
        deps = set()
        for x in res:
            lw = self.lastw.get(x)
            if lw is not None:
                deps.add(lw)
        self.ops.append({'eng': eng, 'fn': None, 'deps': deps, 'dma': None, 'sig': False})

    def emit(self, nc, es, upto=None):
        ops = self.ops
        if upto is not None:
            ops = ops[:upto]
            last = {}
            for i, op in enumerate(ops):
                last[op['dma'] if op['dma'] is not None else op['eng']] = i
            for eng in ('pe', 'act', 'dve', 'pool'):
                ops.append({'eng': eng, 'fn': self.nopfn.get(eng), 'deps': set(), 'dma': None, 'sig': False})
            ops.append({'eng': 'sp', 'fn': None, 'deps': set(last.values()), 'dma': None, 'sig': False})
            self.ops = ops
        for op in ops:
            op['waits'] = []
        for op in ops:
            for d in op['deps']:
                p = ops[d]
                if p['dma'] is None and p['eng'] == 'pe' and op['eng'] == 'pe':
                    continue
                p['sig'] = True
                op['waits'].append(d)
        cnt = {}
        for op in ops:
            if op['sig']:
                key = op['dma'] if op['dma'] is not None else op['eng']
                inc = 16 if op['dma'] is not None else 1
                cnt[key] = cnt.get(key, 0) + inc
                op['cnt'] = cnt[key]
                op['key'] = key
        sems = {}
        for key in cnt:
            sems[key] = es.enter_context(nc.semaphore("s_" + str(key)))
        blk = es.enter_context(nc.Block())
        for ename, attr in self.ENG.items():
            mine = [op for op in ops if op['eng'] == ename]
            if not mine:
                continue

            def body(e, mine=mine):
                waited = {}
                for op in mine:
                    need = {}
                    for d in op['waits']:
                        p = ops[d]
                        need[p['key']] = max(need.get(p['key'], 0), p['cnt'])
                    for k, v in need.items():
                        if waited.get(k, 0) < v:
                            e.wait_ge(sems[k], v)
                            waited[k] = v
                    if op['fn'] is not None:
                        ins = op['fn'](e)
                        if op['sig']:
                            ins.then_inc(sems[op['key']], 16 if op['dma'] is not None else 1)

            getattr(blk, attr)(body)


def interleave(gens):
    gens = list(gens)
    while gens:
        nxt = []
        for g in gens:
            try:
                next(g)
                nxt.append(g)
            except StopIteration:
                pass
        gens = nxt


def build(cfg):
    D, T, KC, NH, NT, NB, PLE = cfg.D, cfg.T, cfg.KC, cfg.NH, cfg.NT, cfg.NB, cfg.PLE
    T2 = 2 * T
    NT2 = 2 * NT
    KP = PLE // 128
    NG = 9
    NOB = D // 256
    NGB = D // 512
    nc = bass.Bass("TRN2", target_bir_lowering=False)

    def din(name, shape):
        return nc.dram_tensor(name, list(shape), F32, kind="ExternalInput").ap()

    xl = din("xl", [T2, D])
    pl = din("pl", [T, PLE])
    wu_in = din("wu_in", [NG * NH, 128, KC * 128])
    wu_out = din("wu_out", [2 * NGB, 128, KC * 512])
    wu_g = din("wu_g", [NGB, 128, KC * 512])
    wu_p = din("wu_p", [NGB, 128, KP * 512])
    chvec = din("chvec", [128, 9 * NH])
    bcv = din("bcv", [2, 128, D])
    cst = din("cst", [5, 128, 512])
    y = nc.dram_tensor("y", [T, D], F32, kind="ExternalOutput").ap()
    dbg = {}
    if cfg.debug:
        for nm, shp in (("d_hnT", [128, KC * T2]), ("d_sinit", [128, NH * 128]),
                        ("d_omix", [128, 2 * KC * T]), ("d_h2", [128, NT * D])):
            dbg[nm] = nc.dram_tensor(nm, shp, F32, kind="ExternalOutput").ap()

    P = Prog()
    es = ExitStack()

    TOTW = nc.sbuf_bytes_remaining // 4 - 64
    big = es.enter_context(nc.sbuf_tensor("big", [128, TOTW], F32))
    ps = [es.enter_context(nc.psum_tensor("ps%d" % i, [128, 512], F32))[:, :] for i in range(8)]
    psb = [p_.bitcast(BF16) for p_ in ps]

    class Carve:
        def __init__(self, start=0):
            self.off = start

        def f32(self, n):
            o = self.off
            self.off += n
            assert self.off <= TOTW, ("SBUF overflow", self.off, TOTW)
            return big[:, o:o + n]

        def bf(self, n):
            w = (n + 1) // 2
            o = self.off
            self.off += w
            assert self.off <= TOTW, ("SBUF overflow", self.off, TOTW)
            return big[:, o:o + w].bitcast(BF16)

    cv = Carve()
    ident = cv.bf(128)
    mask_o = cv.bf(512)
    mask_i = cv.bf(512)
    rmask = cv.f32(512)
    ones_f = cv.f32(512)
    cstage = cv.f32(512)
    chv = cv.f32(9 * NH)
    lbv = cv.f32(2 * NH)
    omlv = cv.f32(2 * NH)
    nomlv = cv.f32(2 * NH)
    epsv = cv.f32(1)
    mcol = cv.f32(2)
    sinit = cv.f32(NH * 128)
    rstdB = cv.f32(NT)
    small = cv.f32(64)
    halo = cv.bf(KC * 2)
    regB = cv.bf(KC * T)
    omixA = cv.bf(KC * T)
    phase4_base = cv.off
    hnT_own = cv.bf(KC * T)
    phase_base = cv.off

    def hn_own(kc, a, b):
        return hnT_own[:, kc * T + a: kc * T + b]

    def hn_par(kc, a, b):
        return regB[:, kc * T + a: kc * T + b]

    def hn_any(kc, a, b):
        if a >= T:
            return hn_par(kc, a - T, b - T)
        return hn_own(kc, a, b)

    def hn_res(a):
        return 'hnT_par' if a >= T else 'hnT_own'

    P.nopfn = {
        'pe': lambda e: e.matmul(ps[7][:, 0:2], lhsT=ident, rhs=ident[:, 0:2], start=True, stop=True),
        'act': lambda e: e.activation(out=small[:, 62:63], in_=epsv, func=AF.Copy),
        'dve': lambda e: e.memset(small[:, 63:64], 0.0),
        'pool': lambda e: e.memset(small[:, 59:60], 0.0),
    }
    dmac = [0]

    def wdma(out, in_, r, w, key):
        P.dma('pool', out, in_, r=r, w=w, sem=key)

    def act_sigmoid(out_ap, in_ap, rres, wres):
        P.act(lambda e: e.activation(out=out_ap, in_=in_ap, func=AF.Exp, scale=-1.0), r=list(rres), w=list(wres))
        P.act(lambda e: e.activation(out=out_ap, in_=out_ap, func=AF.Ln, bias=ones_f[0:out_ap.shape[0], 0:1], scale=1.0),
              r=list(wres) + ['ones_f'], w=list(wres))
        P.act(lambda e: e.activation(out=out_ap, in_=out_ap, func=AF.Exp, scale=-1.0), r=list(wres), w=list(wres))

    P.dma('sp', cstage, cst[0], w=['cstage'], sem='c0')
    P.dve(lambda e: e.tensor_copy(out=ident, in_=cstage[:, 0:128]), r=['cstage'], w=['ident'])
    P.dma('sp', cstage, cst[1], r=[], w=['cstage'], sem='c0')
    P.dve(lambda e: e.tensor_copy(out=mask_o, in_=cstage), r=['cstage'], w=['mask_o'])
    P.dma('sp', cstage, cst[2], w=['cstage'], sem='c0')
    P.dve(lambda e: e.tensor_copy(out=mask_i, in_=cstage), r=['cstage'], w=['mask_i'])
    P.dma('sp', rmask, cst[3], w=['rmask'], sem='c1')
    P.dma('sp', ones_f, cst[4], w=['ones_f'], sem='c1')
    P.dma('sp', chv, chvec, w=['chv'], sem='c1')
    P.dve(lambda e: e.memset(epsv, EPS), w=['epsv'])
    P.dve(lambda e: e.memset(mcol[0:64, 0:1], 1.0), w=['mcol'])
    P.dve(lambda e: e.memset(mcol[64:128, 0:1], 0.0), w=['mcol'])
    P.dve(lambda e: e.memset(mcol[0:64, 1:2], 0.0), w=['mcol'])
    P.dve(lambda e: e.memset(mcol[64:128, 1:2], 1.0), w=['mcol'])
    for d in range(2):
        t0 = chv[:, (2 * d) * NH:(2 * d + 1) * NH]
        t1 = chv[:, (2 * d + 1) * NH:(2 * d + 2) * NH]
        lo = lbv[:, d * NH:(d + 1) * NH]
        oo = omlv[:, d * NH:(d + 1) * NH]
        sm = small[:, d * NH:(d + 1) * NH] if 2 * NH <= 64 else None
        P.dve(lambda e, t0=t0, t1=t1, sm=sm: e.tensor_tensor(out=sm, in0=t0, in1=t1, op=ALU.subtract),
              r=['chv'], w=['small'])
        act_sigmoid(lo, sm, ['small'], ['lbv'])
        P.dve(lambda e, lo=lo, oo=oo: e.tensor_scalar(out=oo, in0=lo, scalar1=-1.0, scalar2=1.0,
                                                     op0=ALU.mult, op1=ALU.add), r=['lbv'], w=['omlv'])
        no = nomlv[:, d * NH:(d + 1) * NH]
        P.dve(lambda e, lo=lo, no=no: e.tensor_scalar(out=no, in0=lo, scalar1=1.0, scalar2=-1.0,
                                                     op0=ALU.mult, op1=ALU.add), r=['lbv'], w=['nomlv'])

    def chcol(idx, h):
        return chv[:, idx * NH + h: idx * NH + h + 1]

    def rstd_from_ss(ss_ap, out_ap, n, rres, wres, tmp_ap):
        P.act(lambda e: e.activation(out=tmp_ap, in_=ss_ap, func=AF.Ln, bias=epsv, scale=1.0 / n),
              r=list(rres) + ['epsv'], w=['rs_tmp'])
        P.act(lambda e: e.activation(out=out_ap, in_=tmp_ap, func=AF.Exp, scale=-0.5),
              r=['rs_tmp'], w=list(wres))

    c1 = Carve(phase_base)
    nwbc = c1.f32(D)
    NXB = 4
    xb = [c1.f32(D) for _ in range(NXB)]
    sqb = c1.f32(D)
    hnb = [c1.bf(D) for _ in range(NXB)]
    ssb = c1.f32(2 * NXB)
    P.dma('sp', nwbc, bcv[0], w=['nwbc'], sem='c1')
    for tt in range(NT2):
        s = tt % NXB
        xs, hs = xb[s], hnb[s]
        P.dma('sp', xs, xl[tt * 128:(tt + 1) * 128, :], w=['xb%d' % s], sem='x%d' % s)
        P.act(lambda e, xs=xs: e.activation(out=sqb, in_=xs, func=AF.Square), r=['xb%d' % s], w=['sqb'])
        ss1 = ssb[:, s:s + 1]
        rs1 = ssb[:, NXB + s:NXB + s + 1]
        P.dve(lambda e, ss1=ss1: e.tensor_reduce(out=ss1, in_=sqb, axis=AX.X, op=ALU.add), r=['sqb'], w=['ss%d' % s])
        rstd_from_ss(ss1, rs1, D, ['ss%d' % s], ['rs%d' % s], small[:, 60:61])
        P.dve(lambda e, xs=xs, hs=hs, rs1=rs1: e.scalar_tensor_tensor(out=hs, in0=xs, scalar=rs1, in1=nwbc,
                                                                     op0=ALU.mult, op1=ALU.mult),
              r=['xb%d' % s, 'rs%d' % s, 'nwbc'], w=['hnb%d' % s])
        for g in range((KC + 7) // 8):
            bank = (2 * s + (g % 2)) % 8
            n = min(8, KC - 8 * g)
            for j in range(n):
                kc = 8 * g + j
                P.pe(lambda e, bank=bank, j=j, kc=kc, hs=hs: e.transpose(
                    out=psb[bank][:, j * 128:(j + 1) * 128], in_=hs[:, kc * 128:(kc + 1) * 128], identity=ident),
                    r=['hnb%d' % s, 'ident'], w=['ps%d' % bank])
            a = tt * 128
            base = hnT_own if a < T else regB
            aa = a if a < T else a - T
            dst = base.rearrange("p (k t) -> p k t", t=T)[:, 8 * g:8 * g + n, aa:aa + 128]
            src = psb[bank][:, 0:n * 128].rearrange("p (k t) -> p k t", t=128)
            eng = P.act if g % 2 == 0 else P.dve
            if g % 2 == 0:
                P.act(lambda e, dst=dst, src=src: e.activation(out=dst, in_=src, func=AF.Copy),
                      r=['ps%d' % bank], w=[hn_res(a)])
            else:
                P.dve(lambda e, dst=dst, src=src: e.tensor_copy(out=dst, in_=src),
                      r=['ps%d' % bank], w=[hn_res(a)])
    P.dve(lambda e: e.tensor_copy(out=halo.rearrange("p (k c) -> p k c", c=2),
                                  in_=regB.rearrange("p (k t) -> p k t", t=T)[:, :, 0:2]),
          r=['hnT_par'], w=['halo'])
    if cfg.debug:
        P.dma('pool', dbg['d_hnT'][:, 0:KC * T], hnT_own, r=['hnT_own'], w=['dbg0'], sem='dbg')
        P.dma('pool', dbg['d_hnT'][:, KC * T:2 * KC * T], regB, r=['hnT_par'], w=['dbg0'], sem='dbg')
    P.barrier()

    def proj_fm(bank, wslot, wres, col0, tok0, ncols=512):
        gi_ = col0 // 128
        for kc in range(KC):
            lh = wslot[:, (gi_ * KC + kc) * 128:(gi_ * KC + kc + 1) * 128]
            rh = hn_any(kc, tok0, tok0 + ncols)
            P.pe(lambda e, kc=kc, lh=lh, rh=rh: e.matmul(ps[bank][:, 0:ncols], lhsT=lh, rhs=rh,
                                                         start=(kc == 0), stop=(kc == KC - 1)),
                 r=[wres, hn_res(tok0)], w=['ps%d' % bank])

    wslot_w = [0]

    c2 = Carve(phase_base)
    wslot_w[0] = 256
    wB = [c2.bf(KC * 256) for _ in range(2)]
    b_t2 = [c2.f32(512) for _ in range(2)]
    b_lfp = [c2.f32(1 + T) for _ in range(2)]
    b_kk = [c2.f32(T) for _ in range(2)]
    b_e = [c2.f32(T) for _ in range(2)]
    b_ones = c2.f32(T)
    b_kw = [c2.bf(T) for _ in range(2)]
    b_kwtok = [c2.bf(T) for _ in range(2)]
    b_vtok = [c2.bf(T) for _ in range(2)]
    for q_ in range(2):
        P.dve(lambda e, q_=q_: e.memset(b_lfp[q_][:, 0:1], 0.0), w=['b_lfp%d' % q_])
    P.dve(lambda e: e.memset(b_ones, 1.0), w=['b_ones'])

    def b0_load(h):
        s = h % 2
        for gi, g in enumerate((1, 3)):
            wdma(wB[s][:, gi * KC * 128:(gi + 1) * KC * 128], wu_in[g * NH + h], r=[], w=['wB%d' % s], key='wB%d' % s)

    def b0_head(h):
        s = h % 2
        w_ = wB[s]
        wres = 'wB%d' % s
        lb1 = lbv[:, NH + h:NH + h + 1]
        om1 = omlv[:, NH + h:NH + h + 1]
        t2, lfp, kk_, e_, kw, kwtok, vtk = b_t2[s], b_lfp[s], b_kk[s], b_e[s], b_kw[s], b_kwtok[s], b_vtok[s]
        for tb in range(NB):
            bank = tb % 2
            proj_fm(bank, w_, wres, 128, T + tb * 512)
            tq = b_t2[tb % 2] if False else t2
            act_sigmoid(t2, ps[bank], ['ps%d' % bank], ['b_t2%d' % s])
            P.dve(lambda e: e.tensor_scalar(out=t2, in0=t2, scalar1=om1, scalar2=lb1, op0=ALU.mult, op1=ALU.add),
                  r=['b_t2%d' % s, 'omlv', 'lbv'], w=['b_t2%d' % s])
            P.act(lambda e, tb=tb: e.activation(out=lfp[:, 1 + tb * 512:1 + (tb + 1) * 512], in_=t2, func=AF.Ln),
                  r=['b_t2%d' % s], w=['b_lfp%d' % s])
            P.pool(lambda e, tb=tb: e.tensor_scalar(out=kk_[:, tb * 512:(tb + 1) * 512], in0=t2, scalar1=-1.0, scalar2=1.0,
                                                    op0=ALU.mult, op1=ALU.add), r=['b_t2%d' % s], w=['b_kk%d' % s])
        for g in range((NT + 3) // 4):
            n = min(4, NT - 4 * g)
            vb = 4 + (g % 2)
            for j in range(n):
                tt = 4 * g + j
                for kc in range(KC):
                    P.pe(lambda e, j=j, tt=tt, kc=kc, vb=vb: e.matmul(ps[vb][:, j * 128:(j + 1) * 128],
                                                                      lhsT=hn_par(kc, tt * 128, (tt + 1) * 128),
                                                                      rhs=w_[:, kc * 128:(kc + 1) * 128],
                                                                      start=(kc == 0), stop=(kc == KC - 1)),
                         r=[wres, 'hnT_par'], w=['ps%d' % vb])
            P.act(lambda e, g=g, n=n, vb=vb: e.activation(out=vtk[:, g * 512:g * 512 + n * 128], in_=ps[vb][:, 0:n * 128],
                                                          func=AF.Copy), r=['ps%d' % vb], w=['b_vtok%d' % s])
        P.dve(lambda e: e.tensor_tensor_scan(out=e_, data0=lfp[:, 0:T], data1=b_ones, initial=0.0,
                                             op0=ALU.add, op1=ALU.mult), r=['b_lfp%d' % s, 'b_ones'], w=['b_e%d' % s])
        P.act(lambda e: e.activation(out=e_, in_=e_, func=AF.Exp), r=['b_e%d' % s], w=['b_e%d' % s])
        P.dve(lambda e: e.tensor_tensor(out=kw, in0=kk_, in1=e_, op=ALU.mult), r=['b_kk%d' % s, 'b_e%d' % s], w=['b_kw%d' % s])
        tbk = 2 + s
        for g in range(NT // 8 if NT >= 8 else 1):
            n = min(8, NT - 8 * g)
            for j in range(n):
                tt = 8 * g + j
                P.pe(lambda e, j=j, tt=tt: e.transpose(out=psb[tbk][:, j * 128:(j + 1) * 128],
                                                       in_=kw[:, tt * 128:(tt + 1) * 128], identity=ident),
                     r=['b_kw%d' % s, 'ident'], w=['ps%d' % tbk])
            P.dve(lambda e, g=g, n=n: e.tensor_copy(out=kwtok[:, g * 1024:g * 1024 + n * 128], in_=psb[tbk][:, 0:n * 128]),
                  r=['ps%d' % tbk], w=['b_kwtok%d' % s])
        sb = 6 + s
        for tt in range(NT):
            P.pe(lambda e, tt=tt: e.matmul(ps[sb][:, 0:128], lhsT=kwtok[:, tt * 128:(tt + 1) * 128],
                                           rhs=vtk[:, tt * 128:(tt + 1) * 128], start=(tt == 0), stop=(tt == NT - 1)),
                 r=['b_kwtok%d' % s, 'b_vtok%d' % s], w=['ps%d' % sb])
        P.dve(lambda e: e.tensor_copy(out=sinit[:, h * 128:(h + 1) * 128], in_=ps[sb][:, 0:128]), r=['ps%d' % sb], w=['sinit'])

    b0_load(0)
    if NH > 1:
        b0_load(1)
    for h in range(NH):
        b0_head(h)
        if h + 2 < NH:
            b0_load(h + 2)
    if cfg.debug:
        P.dma('sp', dbg['d_sinit'], sinit, r=['sinit'], w=['dbg1'], sem='dbg1')
    P.barrier()

    c3 = Carve(phase_base)
    wslot_w[0] = 640
    wA = [c3.bf(KC * 640) for _ in range(2)]
    cB = Carve(phase_base)
    so_words = 0

    class RB:
        off = 0

    def rb_bf(n):
        if 12 * T > KC * T:
            return c3.bf(n)
        o = RB.off
        RB.off += n
        assert RB.off <= KC * T
        return regB[:, o:o + n]

    qrel = [[rb_bf(T) for _ in range(2)] for _ in range(2)]
    krelT = [[rb_bf(T) for _ in range(2)] for _ in range(2)]
    vtok = [rb_bf(T) for _ in range(2)]
    zs = [rb_bf(T) for _ in range(2)]
    NCK = T // CH
    scal = [[c3.f32(3 * NCK) for _ in range(2)] for _ in range(2)]
    a_qs = c3.f32(512)
    a_qs2 = [a_qs, cstage]
    rbrem = regB.bitcast(F32)
    if 12 * T <= KC * T and (KC * T - 12 * T) // 2 >= 4 * 512:
        _o = 12 * T // 2
        xtra = [rbrem[:, _o + i * 512:_o + (i + 1) * 512] for i in range(4)]
    else:
        xtra = [c3.f32(512) for _ in range(4)]
    a_t2 = [c3.f32(512), xtra[0]]
    a_t3 = [c3.f32(512), xtra[1]]
    a_lfp = [c3.f32(513), c3.f32(513)]
    a_c = [c3.f32(512), xtra[2]]
    a_a = [c3.f32(512), xtra[3]]
    a_qm = c3.f32(512)
    a_km = c3.f32(512)
    a_tt = c3.f32(2 * 8)
    g_ktok = [c3.bf(T) for _ in range(2)]
    g_ktokh = [c3.bf(T) for _ in range(2)]
    g_scm = [c3.bf(T) for _ in range(2)]
    KVR = min(8, NCK)
    g_kvs = [c3.f32(KVR * 128) for _ in range(2)]
    g_sbf = [c3.bf(NCK * 128) for _ in range(2)]
    g_S4 = [c3.f32(128) for _ in range(4)]
    g_osq = c3.f32(512)
    g_rs = c3.f32(512)
    g_on = g_osq
    g_ln = g_rs
    for q_ in range(2):
        P.dve(lambda e, q_=q_: e.memset(a_lfp[q_][:, 0:1], 0.0), w=['a_lfp%d' % q_])

    def a_load(h):
        s = h % 2
        for g in range(5):
            wdma(wA[s][:, g * KC * 128:(g + 1) * KC * 128], wu_in[g * NH + h], r=[], w=['wA%d' % s], key='wA%d_%d' % (s, g % 2))

    def a_proj(h):
        s = h % 2
        w_ = wA[s]
        wres = 'wA%d' % s
        pend = [None]
        for tb in range(NB):
            t0 = tb * 512
            nck = 512 // CH
            qs_ = a_qs2[tb % 2]
            qsr = 'a_qs%d' % (tb % 2)
            proj_fm(0, w_, wres, 0, t0)
            act_sigmoid(qs_, ps[0], ['ps0'], [qsr])
            P.dve(lambda e, qs_=qs_: e.tensor_tensor(out=qs_, in0=ps[0], in1=qs_, op=ALU.mult), r=['ps0', qsr], w=[qsr])
            yield
            proj_fm(1, w_, wres, 4 * 128, t0)
            act_sigmoid(a_km, ps[1], ['ps1'], ['a_km'])
            P.dve(lambda e, t0=t0: e.tensor_tensor(out=zs[s][:, t0:t0 + 512], in0=ps[1], in1=a_km, op=ALU.mult),
                  r=['ps1', 'a_km'], w=['zs%d' % s])
            yield
            def it_body(d, tb=tb, t0=t0, nck=nck, qs_=qs_, qsr=qsr):
                bank = d % 2
                lb1 = lbv[:, d * NH + h:d * NH + h + 1]
                om1 = omlv[:, d * NH + h:d * NH + h + 1]
                proj_fm(bank, w_, wres, (2 + d) * 128, t0)
                nom1 = nomlv[:, d * NH + h:d * NH + h + 1]
                t2_, t3_, lfp_, c_, aa_ = a_t2[d], a_t3[d], a_lfp[d], a_c[d], a_a[d]
                act_sigmoid(t2_, ps[bank], ['ps%d' % bank], ['a_t2%d' % d])
                P.act(lambda e, t2_=t2_, lfp_=lfp_, lb1=lb1, om1=om1: e.activation(out=lfp_[:, 1:513], in_=t2_, func=AF.Ln,
                                                                                   bias=lb1, scale=om1),
                      r=['a_t2%d' % d, 'lbv', 'omlv'], w=['a_lfp%d' % d])
                P.pool(lambda e, t2_=t2_, t3_=t3_, nom1=nom1, om1=om1: e.tensor_scalar(out=t3_, in0=t2_, scalar1=nom1, scalar2=om1,
                                                                                     op0=ALU.mult, op1=ALU.add),
                       r=['a_t2%d' % d, 'omlv', 'nomlv'], w=['a_t3%d' % d])
                if d == 0:
                    P.dve(lambda e, c_=c_, lfp_=lfp_: e.tensor_tensor_scan(out=c_, data0=rmask, data1=lfp_[:, 1:513], initial=0.0,
                                                                         op0=ALU.mult, op1=ALU.add),
                          r=['rmask', 'a_lfp%d' % d], w=['a_c%d' % d])
                    ref = 31
                else:
                    P.dve(lambda e, c_=c_, lfp_=lfp_: e.tensor_tensor_scan(out=c_, data0=lfp_[:, 0:512], data1=rmask, initial=0.0,
                                                                         op0=ALU.add, op1=ALU.mult),
                          r=['rmask', 'a_lfp%d' % d], w=['a_c%d' % d])
                    ref = 32
                c3v = c_.rearrange("p (n c) -> p n c", c=CH)
                a3v = aa_.rearrange("p (n c) -> p n c", c=CH)
                P.dve(lambda e, c3v=c3v, a3v=a3v, ref=ref: e.tensor_tensor(
                    out=a3v, in0=c3v, in1=c3v[:, :, ref:ref + 1].to_broadcast([128, nck, CH]), op=ALU.subtract),
                    r=['a_c%d' % d], w=['a_a%d' % d])
                yield
                sc_ = scal[s][d]
                dec = sc_[:, 0 * NCK + tb * nck:0 * NCK + (tb + 1) * nck]
                gg = sc_[:, 1 * NCK + tb * nck:1 * NCK + (tb + 1) * nck]
                scc = sc_[:, 2 * NCK + tb * nck:2 * NCK + (tb + 1) * nck]
                sres = 'scal%d_%d' % (s, d)
                c_last = c3v[:, :, CH - 1:CH].rearrange("p n o -> p (n o)")
                c_ref = c3v[:, :, ref:ref + 1].rearrange("p n o -> p (n o)")
                a_last = a3v[:, :, CH - 1:CH].rearrange("p n o -> p (n o)")
                if d == 0:
                    P.act(lambda e, dec=dec, c_last=c_last: e.activation(out=dec, in_=c_last, func=AF.Exp),
                          r=['a_c%d' % d], w=[sres])
                    P.act(lambda e, scc=scc, c_ref=c_ref: e.activation(out=scc, in_=c_ref, func=AF.Exp),
                          r=['a_c%d' % d], w=[sres])
                    P.act(lambda e, gg=gg, a_last=a_last: e.activation(out=gg, in_=a_last, func=AF.Exp),
                          r=['a_a%d' % d], w=[sres])
                else:
                    lf_last = lfp_[:, 1:513].rearrange("p (n c) -> p n c", c=CH)[:, :, CH - 1:CH].rearrange("p n o -> p (n o)")
                    tt_ = a_tt[:, 0:nck]
                    tt2 = a_tt[:, 8:8 + nck]
                    P.dve(lambda e, tt_=tt_, c_last=c_last, lf_last=lf_last: e.tensor_tensor(
                        out=tt_, in0=c_last, in1=lf_last, op=ALU.add), r=['a_c%d' % d, 'a_lfp%d' % d], w=['a_tt'])
                    P.dve(lambda e, tt_=tt_, tt2=tt2, c_ref=c_ref: e.tensor_tensor(
                        out=tt2, in0=tt_, in1=c_ref, op=ALU.subtract), r=['a_c%d' % d, 'a_tt'], w=['a_tt2'])
                    P.act(lambda e, dec=dec, tt_=tt_: e.activation(out=dec, in_=tt_, func=AF.Exp), r=['a_tt'], w=[sres])
                    P.act(lambda e, gg=gg, c_ref=c_ref: e.activation(out=gg, in_=c_ref, func=AF.Exp), r=['a_c%d' % d], w=[sres])
                    P.act(lambda e, scc=scc, tt2=tt2: e.activation(out=scc, in_=tt2, func=AF.Exp), r=['a_tt2'], w=[sres])
                sq = 1.0 if d == 0 else -1.0
                P.act(lambda e, sq=sq, aa_=aa_: e.activation(out=a_qm, in_=aa_, func=AF.Exp, scale=sq), r=['a_a%d' % d], w=['a_qm'])
                P.act(lambda e, sq=sq, aa_=aa_: e.activation(out=a_km, in_=aa_, func=AF.Exp, scale=-sq), r=['a_a%d' % d], w=['a_km'])
                P.dve(lambda e, d=d, t0=t0: e.tensor_tensor(out=qrel[s][d][:, t0:t0 + 512], in0=qs_, in1=a_qm, op=ALU.mult),
                      r=[qsr, 'a_qm'], w=['qrel%d_%d' % (s, d)])
                P.pool(lambda e, d=d, t0=t0, t3_=t3_: e.tensor_tensor(out=krelT[s][d][:, t0:t0 + 512], in0=t3_, in1=a_km, op=ALU.mult),
                       r=['a_t3%d' % d, 'a_km'], w=['krelT%d_%d' % (s, d)])
            for d in range(2):
                g_ = it_body(d)
                next(g_)
                if pend[0] is not None:
                    for _ in pend[0]:
                        pass
                pend[0] = g_
                yield
            for j in range(4):
                tt = tb * 4 + j
                if j == 2:
                    yield
                for kc in range(KC):
                    P.pe(lambda e, j=j, tt=tt, kc=kc: e.matmul(ps[2][:, j * 128:(j + 1) * 128],
                                                               lhsT=hn_own(kc, tt * 128, (tt + 1) * 128),
                                                               rhs=w_[:, (KC + kc) * 128:(KC + kc + 1) * 128],
                                                               start=(kc == 0), stop=(kc == KC - 1)),
                         r=[wres, 'hnT_own'], w=['ps2'])
            P.dve(lambda e, t0=t0: e.tensor_copy(out=vtok[s][:, t0:t0 + 512], in_=ps[2]), r=['ps2'], w=['vtok%d' % s])
            yield
        if pend[0] is not None:
            for _ in pend[0]:
                pass
        yield


    def a_gla(h):
        s = h % 2
        masks = (mask_o, mask_i)
        for d in range(2):
            kT = krelT[s][d]
            qr = qrel[s][d]
            sres = 'scal%d_%d' % (s, d)
            sc_ = scal[s][d]
            if d == 0:
                P.dve(lambda e: e.memset(g_S4[0], 0.0), w=['g_S0'])
            else:
                P.dve(lambda e: e.tensor_copy(out=g_S4[0], in_=sinit[:, h * 128:(h + 1) * 128]), r=['sinit'], w=['g_S0'])
            si = [0]
            gorder = list(range(NT // 4)) if d == 0 else list(range(NT // 4 - 1, -1, -1))
            for g in gorder:
                for j in range(4):
                    tt = 4 * g + j
                    P.pe(lambda e, j=j, tt=tt, kT=kT: e.transpose(out=psb[3][:, j * 128:(j + 1) * 128],
                                                                  in_=kT[:, tt * 128:(tt + 1) * 128], identity=ident),
                         r=['krelT%d_%d' % (s, d), 'ident'], w=['ps3'])
                P.dve(lambda e, g=g, d=d: e.tensor_scalar(out=g_ktok[d][:, g * 512:(g + 1) * 512], in0=psb[3][:, 0:512],
                                                          scalar1=mcol[:, 0:1], scalar2=None, op0=ALU.mult),
                      r=['ps3', 'mcol'], w=['g_ktok%d' % d])
                P.dve(lambda e, g=g, d=d: e.tensor_scalar(out=g_ktokh[d][:, g * 512:(g + 1) * 512], in0=psb[3][:, 0:512],
                                                          scalar1=mcol[:, 1:2], scalar2=None, op0=ALU.mult),
                      r=['ps3', 'mcol'], w=['g_ktok%d' % d])
                yield
                for j in range(4):
                    tt = 4 * g + j
                    P.pe(lambda e, j=j, tt=tt, kT=kT, qr=qr: e.matmul(ps[4][:, j * 128:(j + 1) * 128],
                                                                      lhsT=kT[:, tt * 128:(tt + 1) * 128],
                                                                      rhs=qr[:, tt * 128:(tt + 1) * 128], start=True, stop=True),
                         r=['krelT%d_%d' % (s, d), 'qrel%d_%d' % (s, d)], w=['ps4'])
                P.dve(lambda e, g=g, d=d: e.tensor_tensor(out=g_scm[d][:, g * 512:(g + 1) * 512], in0=ps[4], in1=masks[d],
                                                          op=ALU.mult), r=['ps4', 'mask_o', 'mask_i'], w=['g_scm%d' % d])
                yield
                for half in range(2):
                    for jj in range(4):
                        n = 8 * g + 4 * half + jj
                        tt = n // 2
                        r0 = (n % 2) * 64
                        P.pe(lambda e, jj=jj, tt=tt, r0=r0, d=d: e.matmul(
                            ps[5][:, jj * 128:(jj + 1) * 128],
                            lhsT=(g_ktok[d] if r0 == 0 else g_ktokh[d])[:, tt * 128:(tt + 1) * 128],
                            rhs=vtok[s][:, tt * 128:(tt + 1) * 128], start=True, stop=True),
                            r=['g_ktok%d' % d, 'vtok%d' % s], w=['ps5'])
                    if half == 1:
                        yield
                    n0 = 8 * g + 4 * half
                    sl0 = (n0 % KVR) * 128
                    gsl = sc_[:, NCK + n0:NCK + n0 + 4]
                    P.dve(lambda e, d=d, sl0=sl0, gsl=gsl: e.tensor_tensor(
                        out=g_kvs[d][:, sl0:sl0 + 512].rearrange("p (n c) -> p n c", c=128),
                        in0=ps[5].rearrange("p (n c) -> p n c", c=128),
                        in1=gsl.unsqueeze(2).to_broadcast([128, 4, 128]), op=ALU.mult),
                        r=['ps5', sres], w=['g_kvs%d_%d' % (d, (n0 + q_) % KVR) for q_ in range(4)])
                yield
                order = range(8 * g, 8 * g + 8) if d == 0 else range(8 * g + 7, 8 * g - 1, -1)
                for n in order:
                    cur = si[0] % 4
                    nx = (si[0] + 1) % 4
                    si[0] += 1
                    P.act(lambda e, n=n, d=d, sc_=sc_, cur=cur: e.activation(
                        out=g_sbf[d][:, n * 128:(n + 1) * 128], in_=g_S4[cur], func=AF.Identity,
                        scale=sc_[:, 2 * NCK + n:2 * NCK + n + 1]), r=['g_S%d' % cur, sres], w=['g_sbf%d' % d])
                    P.dve(lambda e, n=n, d=d, sc_=sc_, cur=cur, nx=nx: e.scalar_tensor_tensor(
                        out=g_S4[nx], in0=g_S4[cur], scalar=sc_[:, n:n + 1], in1=g_kvs[d][:, (n % KVR) * 128:(n % KVR + 1) * 128],
                        op0=ALU.mult, op1=ALU.add), r=['g_S%d' % cur, 'g_kvs%d_%d' % (d, n % KVR), sres], w=['g_S%d' % nx])
                    if n % 2 == 1:
                        yield
                yield
        for g in range(NT // 4):
            ob = 6 if g % 2 == 0 else 3
            for j in range(4):
                tt = 4 * g + j
                mm = []
                for d in range(2):
                    mm.append((vtok[s][:, tt * 128:(tt + 1) * 128], g_scm[d][:, tt * 128:(tt + 1) * 128],
                               slice(j * 128, (j + 1) * 128)))
                for c in range(2):
                    n = 2 * tt + c
                    for d in range(2):
                        mm.append((g_sbf[d][:, n * 128:(n + 1) * 128], qrel[s][d][:, n * CH:(n + 1) * CH],
                                   slice(j * 128 + c * CH, j * 128 + (c + 1) * CH)))
                for i_, (l_, r_, sl_) in enumerate(mm):
                    P.pe(lambda e, l_=l_, r_=r_, sl_=sl_, i_=i_, ob=ob, last=(i_ == len(mm) - 1): e.matmul(
                        ps[ob][:, sl_], lhsT=l_, rhs=r_, start=(i_ == 0), stop=last),
                        r=['vtok%d' % s, 'g_scm0', 'g_scm1', 'g_sbf0', 'g_sbf1', 'qrel%d_0' % s, 'qrel%d_1' % s], w=['ps%d' % ob])
            P.act(lambda e, ob=ob: e.activation(out=g_osq, in_=ps[ob], func=AF.Square), r=['ps%d' % ob], w=['g_osq'])
            P.pe(lambda e: e.matmul(ps[7], lhsT=ones_f[:, 0:128], rhs=g_osq, start=True, stop=True),
                 r=['ones_f', 'g_osq'], w=['ps7'])
            P.act(lambda e: e.activation(out=g_ln, in_=ps[7], func=AF.Ln, bias=epsv, scale=1.0 / 128), r=['ps7', 'epsv'], w=['g_rs'])
            P.act(lambda e: e.activation(out=g_rs, in_=g_ln, func=AF.Exp, scale=-0.5), r=['g_rs'], w=['g_rs'])
            P.dve(lambda e, ob=ob: e.tensor_tensor(out=g_on, in0=ps[ob], in1=g_rs, op=ALU.mult), r=['ps%d' % ob, 'g_rs'], w=['g_osq'])
            P.dve(lambda e, g=g: e.scalar_tensor_tensor(out=omixA[:, h * T + g * 512:h * T + (g + 1) * 512], in0=g_on,
                                                        scalar=chcol(4, h), in1=zs[s][:, g * 512:(g + 1) * 512],
                                                        op0=ALU.mult, op1=ALU.mult),
                  r=['g_osq', 'chv', 'zs%d' % s], w=['omixA'])
            yield

    a_load(0)
    if NH > 1:
        a_load(1)
    for h in range(NH + 1):
        gens = []
        if h < NH:
            gens.append(a_proj(h))
        if h >= 1:
            gens.append(a_gla(h - 1))
        interleave(gens)
        if h + 2 < NH:
            a_load(h + 2)
    P.barrier()

    c4 = Carve(phase_base)
    wslot_w[0] = 512
    wC = [c4.bf(KC * 512) for _ in range(2)]
    omixB = regB
    c_cs = c4.f32(512)
    c_u = c4.f32(T + 2)
    c_y = c4.f32(512)
    c_yb = c4.f32(512)
    c_sq = c4.f32(512)
    c_z = c4.f32(512)
    c_h = c4.f32(2)
    c_ss = c4.f32(NT)
    P.dve(lambda e: e.memset(c_ss, 0.0), w=['c_ss'])
    P.dve(lambda e: e.memset(c_u[:, 0:1], 0.0), w=['c_u'])

    def c_load(cb):
        s = cb % 2
        for g in range(4):
            wdma(wC[s][:, g * KC * 128:(g + 1) * KC * 128], wu_in[(5 + g) * NH + cb], r=[], w=['wC%d' % s], key='wC%d_%d' % (s, g % 2))

    def c_block(cb):
        s = cb % 2
        w_ = wC[s]
        wres = 'wC%d' % s
        for tb in range(NB):
            t0 = tb * 512
            proj_fm(0, w_, wres, 1 * 128, t0)
            P.act(lambda e: e.activation(out=c_cs, in_=ps[0], func=AF.Copy), r=['ps0'], w=['c_cs'])
            proj_fm(1, w_, wres, 2 * 128, t0)
            P.dve(lambda e, t0=t0: e.tensor_tensor(out=c_u[:, 1 + t0:1 + t0 + 512], in0=c_cs, in1=ps[1], op=ALU.mult),
                  r=['c_cs', 'ps1'], w=['c_u'])
        for gi, g in enumerate((1, 2)):
            for kc in range(KC):
                P.pe(lambda e, gi=gi, g=g, kc=kc: e.matmul(ps[2][:, gi * 2:gi * 2 + 2], lhsT=w_[:, (g * KC + kc) * 128:(g * KC + kc + 1) * 128],
                                                           rhs=halo[:, kc * 2:kc * 2 + 2], start=(kc == 0), stop=(kc == KC - 1)),
                     r=[wres, 'halo'], w=['ps2'])
        P.act(lambda e: e.activation(out=c_h, in_=ps[2][:, 0:2], func=AF.Copy), r=['ps2'], w=['c_h'])
        P.dve(lambda e: e.tensor_tensor(out=c_u[:, T + 1:T + 2], in0=c_h[:, 0:1], in1=ps[2][:, 2:3], op=ALU.mult),
              r=['c_h', 'ps2'], w=['c_u'])
        for tb in range(NB):
            t0 = tb * 512
            P.dve(lambda e, t0=t0: e.tensor_scalar(out=c_y, in0=c_u[:, t0:t0 + 512], scalar1=chcol(5, cb), scalar2=None,
                                                   op0=ALU.mult), r=['c_u', 'chv'], w=['c_y'])
            P.dve(lambda e, t0=t0: e.scalar_tensor_tensor(out=c_y, in0=c_u[:, t0 + 1:t0 + 513], scalar=chcol(6, cb), in1=c_y,
                                                          op0=ALU.mult, op1=ALU.add), r=['c_u', 'chv', 'c_y'], w=['c_y'])
            P.dve(lambda e, t0=t0: e.scalar_tensor_tensor(out=c_y, in0=c_u[:, t0 + 2:t0 + 514], scalar=chcol(7, cb), in1=c_y,
                                                          op0=ALU.mult, op1=ALU.add), r=['c_u', 'chv', 'c_y'], w=['c_y'])
            proj_fm(3, w_, wres, 0, t0)
            P.dve(lambda e: e.tensor_tensor(out=c_yb, in0=c_y, in1=ps[3], op=ALU.mult), r=['c_y', 'ps3'], w=['c_yb'])
            P.act(lambda e: e.activation(out=c_sq, in_=c_yb, func=AF.Square), r=['c_yb'], w=['c_sq'])
            for j in range(4):
                tt = tb * 4 + j
                P.pe(lambda e, j=j, tt=tt: e.matmul(ps[7][:, tt:tt + 1], lhsT=c_sq[:, j * 128:(j + 1) * 128],
                                                    rhs=ones_f[:, 0:1], start=True, stop=True),
                     r=['c_sq', 'ones_f'], w=['ps7'])
            P.dve(lambda e, tb=tb: e.tensor_tensor(out=c_ss[:, tb * 4:tb * 4 + 4], in0=c_ss[:, tb * 4:tb * 4 + 4],
                                                   in1=ps[7][:, tb * 4:tb * 4 + 4], op=ALU.add), r=['ps7', 'c_ss'], w=['c_ss'])
            proj_fm(4, w_, wres, 3 * 128, t0)
            act_sigmoid(c_z, ps[4], ['ps4'], ['c_z'])
            P.dve(lambda e: e.tensor_tensor(out=c_z, in0=ps[4], in1=c_z, op=ALU.mult), r=['ps4', 'c_z'], w=['c_z'])
            P.dve(lambda e, t0=t0: e.scalar_tensor_tensor(out=omixB[:, cb * T + t0:cb * T + t0 + 512], in0=c_yb,
                                                          scalar=chcol(8, cb), in1=c_z, op0=ALU.mult, op1=ALU.mult),
                  r=['c_yb', 'chv', 'c_z'], w=['omixB'])

    c_load(0)
    if NH > 1:
        c_load(1)
    for cb in range(NH):
        c_block(cb)
        if cb + 2 < NH:
            c_load(cb + 2)
    rstd_from_ss(c_ss, rstdB, D, ['c_ss'], ['rstdB'], small[:, 32:32 + NT])
    if cfg.debug:
        P.dma('pool', dbg['d_omix'][:, 0:KC * T], omixA, r=['omixA'], w=['dbg2'], sem='dbg')
        P.dma('pool', dbg['d_omix'][:, KC * T:2 * KC * T], omixB, r=['omixB'], w=['dbg2'], sem='dbg')
    P.barrier()

    c5 = Carve(phase4_base)
    h2 = c5.f32(NT * D)
    wO = [c5.bf(KC * 512) for _ in range(3)]
    for tt in range(NT):
        P.dma('sp', h2[:, tt * D:(tt + 1) * D], xl[tt * 128:(tt + 1) * 128, :], w=['h2_%d' % tt], sem='xh%d' % (tt % 4))
    ounits = [(j, part) for j in range(NGB) for part in range(2)]

    def o_load(u):
        s = u % 3
        half = (KC // 2) * 512 if KC >= 2 else KC * 512
        wdma(wO[s][:, 0:half], wu_out[u][:, 0:half], r=[], w=['wO%d' % s], key='wO%d' % s)
        if half < KC * 512:
            wdma(wO[s][:, half:KC * 512], wu_out[u][:, half:KC * 512], r=[], w=['wO%d' % s], key='wO%d' % s)

    def o_block(j):
        sa = (2 * j) % 3
        sb_ = (2 * j + 1) % 3
        for tt in range(NT):
            ba = 0 + (tt % 2)
            bb = 2 + (tt % 2)
            for kc in range(KC):
                P.pe(lambda e, kc=kc, ba=ba, tt=tt: e.matmul(ps[ba], lhsT=omixA[:, kc * T + tt * 128:kc * T + (tt + 1) * 128],
                                                             rhs=wO[sa][:, kc * 512:(kc + 1) * 512], start=(kc == 0), stop=(kc == KC - 1)),
                     r=['omixA', 'wO%d' % sa], w=['ps%d' % ba])
            for kc in range(KC):
                P.pe(lambda e, kc=kc, bb=bb, tt=tt: e.matmul(ps[bb], lhsT=omixB[:, kc * T + tt * 128:kc * T + (tt + 1) * 128],
                                                             rhs=wO[sb_][:, kc * 512:(kc + 1) * 512], start=(kc == 0), stop=(kc == KC - 1)),
                     r=['omixB', 'wO%d' % sb_], w=['ps%d' % bb])
            hv = h2[:, tt * D + j * 512:tt * D + (j + 1) * 512]
            P.dve(lambda e, hv=hv, ba=ba: e.tensor_tensor(out=hv, in0=hv, in1=ps[ba], op=ALU.add),
                  r=['ps%d' % ba, 'h2_%d' % tt], w=['h2_%d' % tt])
            P.dve(lambda e, hv=hv, bb=bb, tt=tt: e.scalar_tensor_tensor(out=hv, in0=ps[bb], scalar=rstdB[:, tt:tt + 1], in1=hv,
                                                                        op0=ALU.mult, op1=ALU.add),
                  r=['ps%d' % bb, 'h2_%d' % tt, 'rstdB'], w=['h2_%d' % tt])

    nxt_o = [0]

    def o_prefetch(upto):
        while nxt_o[0] < min(upto, len(ounits)):
            o_load(nxt_o[0])
            nxt_o[0] += 1

    o_prefetch(3)
    for j in range(NGB):
        o_block(j)
        o_prefetch((j + 1) * 2 + 3)
    if cfg.debug:
        P.dma('sp', dbg['d_h2'], h2, r=['h2_%d' % tt for tt in range(NT)], w=['dbg3'], sem='dbg1')
    P.barrier()

    c6 = Carve(c5.off - 3 * (KC * 256))
    h2T = omixA
    pT = regB[:, 0:KP * T]
    rbf = regB.bitcast(F32)
    if KP * T // 2 + 2 * D <= KC * T // 2:
        e_ob = [rbf[:, KP * T // 2 + i * D:KP * T // 2 + (i + 1) * D] for i in range(2)]
    else:
        e_ob = [c6.f32(D) for _ in range(2)]
    KH = max(1, KC // 2)
    NKH = KC // KH
    wG = [c6.bf(KH * 512) for _ in range(3)]
    wP = [c6.bf(KP * 512) for _ in range(2)]
    fnbc = c6.f32(D)
    e_hb = [c6.bf(D) for _ in range(2)]
    e_sq = e_hb[0].bitcast(F32) if False else None
    e_sqbase = c6.off - D
    e_sq = big[:, e_sqbase:e_sqbase + D]
    e_pb = c6.f32(PLE)
    e_pbb = c6.bf(PLE)
    e_gs = [c6.f32(512) for _ in range(2)]
    e_t = [c6.f32(512) for _ in range(2)]
    e_ss = c6.f32(4)
    P.dma('sp', fnbc, bcv[1], w=['fnbc'], sem='c1')
    for tt in range(NT):
        s = tt % 2
        P.act(lambda e, tt=tt, s=s: e.activation(out=e_hb[s], in_=h2[:, tt * D:(tt + 1) * D], func=AF.Copy),
              r=['h2_%d' % tt], w=['e_hb%d' % s])
        for g in range((KC + 7) // 8):
            bank = 2 * s + (g % 2)
            n = min(8, KC - 8 * g)
            for j in range(n):
                kc = 8 * g + j
                P.pe(lambda e, bank=bank, j=j, kc=kc, s=s: e.transpose(out=psb[bank][:, j * 128:(j + 1) * 128],
                                                                       in_=e_hb[s][:, kc * 128:(kc + 1) * 128], identity=ident),
                     r=['e_hb%d' % s, 'ident'], w=['ps%d' % bank])
            dst = h2T.rearrange("p (k t) -> p k t", t=T)[:, 8 * g:8 * g + n, tt * 128:(tt + 1) * 128]
            src_ = psb[bank][:, 0:n * 128].rearrange("p (k t) -> p k t", t=128)
            P.dve(lambda e, dst=dst, src_=src_: e.tensor_copy(out=dst, in_=src_), r=['ps%d' % bank], w=['h2T'])
        P.dma('sp', e_pb, pl[tt * 128:(tt + 1) * 128, :], w=['e_pb'], sem='pp')
        P.act(lambda e: e.activation(out=e_pbb, in_=e_pb, func=AF.Copy), r=['e_pb'], w=['e_pbb'])
        for kc in range(KP):
            P.pe(lambda e, kc=kc: e.transpose(out=psb[4][:, kc * 128:(kc + 1) * 128], in_=e_pbb[:, kc * 128:(kc + 1) * 128],
                                              identity=ident), r=['e_pbb', 'ident'], w=['ps4'])
        dst = pT.rearrange("p (k t) -> p k t", t=T)[:, 0:KP, tt * 128:(tt + 1) * 128]
        src_ = psb[4][:, 0:KP * 128].rearrange("p (k t) -> p k t", t=128)
        P.dve(lambda e, dst=dst, src_=src_: e.tensor_copy(out=dst, in_=src_), r=['ps4'], w=['pT'])

    gunits = [(j, kh) for j in range(NGB) for kh in range(NKH)]

    def g_load(u):
        j, kh = gunits[u]
        s = u % 3
        wdma(wG[s], wu_g[j][:, kh * KH * 512:(kh + 1) * KH * 512], r=[], w=['wG%d' % s], key='wG%d' % s)
        if kh == 0:
            wdma(wP[j % 2], wu_p[j], r=[], w=['wP%d' % (j % 2)], key='wP%d' % (j % 2))

    def g_block(j):
        for tt in range(NT):
            b1 = 5 + (tt % 2)
            b2 = 0 + (tt % 2)
            for kc in range(KC):
                u = j * NKH + kc // KH
                s = u % 3
                kk_ = kc % KH
                P.pe(lambda e, kc=kc, b1=b1, tt=tt, s=s, kk_=kk_: e.matmul(
                    ps[b1], lhsT=h2T[:, kc * T + tt * 128:kc * T + (tt + 1) * 128],
                    rhs=wG[s][:, kk_ * 512:(kk_ + 1) * 512], start=(kc == 0), stop=(kc == KC - 1)),
                    r=['h2T', 'wG%d' % s], w=['ps%d' % b1])
            sp_ = j % 2
            for kc in range(KP):
                P.pe(lambda e, kc=kc, b2=b2, tt=tt, sp_=sp_: e.matmul(
                    ps[b2], lhsT=pT[:, kc * T + tt * 128:kc * T + (tt + 1) * 128],
                    rhs=wP[sp_][:, kc * 512:(kc + 1) * 512], start=(kc == 0), stop=(kc == KP - 1)),
                    r=['pT', 'wP%d' % sp_], w=['ps%d' % b2])
            q_ = tt % 2
            act_sigmoid(e_gs[q_], ps[b1], ['ps%d' % b1], ['e_gs%d' % q_])
            P.dve(lambda e, b2=b2, q_=q_: e.tensor_tensor(out=e_t[q_], in0=ps[b2], in1=e_gs[q_], op=ALU.mult),
                  r=['ps%d' % b2, 'e_gs%d' % q_], w=['e_t%d' % q_])
            hv = h2[:, tt * D + j * 512:tt * D + (j + 1) * 512]
            P.pool(lambda e, hv=hv, q_=q_: e.tensor_tensor(out=hv, in0=hv, in1=e_t[q_], op=ALU.add),
                   r=['e_t%d' % q_, 'h2_%d' % tt], w=['h2_%d' % tt])

    nxt_u = [0]

    def g_prefetch(upto):
        while nxt_u[0] < min(upto, len(gunits)):
            g_load(nxt_u[0])
            nxt_u[0] += 1

    g_prefetch(3)
    for j in range(NGB):
        g_block(j)
        g_prefetch((j + 1) * NKH + 3)
    for tt in range(NT):
        s = tt % 2
        hv = h2[:, tt * D:(tt + 1) * D]
        P.act(lambda e, hv=hv: e.activation(out=e_sq, in_=hv, func=AF.Square), r=['h2_%d' % tt], w=['e_sq', 'e_hb0', 'e_hb1'])
        ss1 = e_ss[:, s:s + 1]
        rs1 = e_ss[:, 2 + s:3 + s]
        P.dve(lambda e, ss1=ss1: e.tensor_reduce(out=ss1, in_=e_sq, axis=AX.X, op=ALU.add), r=['e_sq'], w=['e_ss%d' % s])
        rstd_from_ss(ss1, rs1, D, ['e_ss%d' % s], ['e_rs%d' % s], small[:, 61:62])
        P.dve(lambda e, hv=hv, rs1=rs1, s=s: e.scalar_tensor_tensor(out=e_ob[s], in0=hv, scalar=rs1, in1=fnbc,
                                                                    op0=ALU.mult, op1=ALU.mult),
              r=['h2_%d' % tt, 'e_rs%d' % s, 'fnbc'], w=['e_ob%d' % s])
        P.dma('sp', y[tt * 128:(tt + 1) * 128, :], e_ob[s], r=['e_ob%d' % s], w=['yout%d' % tt], sem='yo%d' % s)
    P.wait_all('sp', ['yout%d' % tt for tt in range(NT)] + ['dbg0', 'dbg1', 'dbg2', 'dbg3'])

    stop = getattr(cfg, 'stop', None)
    P.emit(nc, es, upto=(P.marks[stop] if stop is not None and stop < len(P.marks) else (stop if stop is not None and stop >= 100 else None)))
    es.close()
    return nc


def _units(w, ncols_unit):
    K, N = w.shape
    kc = K // 128
    nu = N // ncols_unit
    a = w.reshape(kc, 128, nu, ncols_unit).transpose(2, 1, 0, 3)
    return np.ascontiguousarray(a).reshape(nu, 128, kc * ncols_unit)


def _chan(v, nh):
    return np.ascontiguousarray(v.reshape(nh, 128).T)


def _consts():
    c = np.zeros((5, 128, 512), np.float32)
    c[0, :, 0:128] = np.eye(128, dtype=np.float32)
    s = np.arange(128)[:, None]
    t = np.arange(128)[None, :]
    same = (s // CH) == (t // CH)
    mo = (same & (s <= t)).astype(np.float32)
    mi = (same & (s >= t)).astype(np.float32)
    c[1] = np.tile(mo, (1, 4))
    c[2] = np.tile(mi, (1, 4))
    rm = np.ones(512, np.float32)
    rm[::CH] = 0.0
    c[3] = rm[None, :]
    c[4] = 1.0
    return c


def make_in_maps(cfg, x, p, norm_w, w_in, lb_theta, hgrn_norm_w, conv_w, conv_norm_w,
                 w_out, w_ple, w_ple_gate, final_norm_w):
    D, T, NH = cfg.D, cfg.T, cfg.NH
    B = x.shape[0]
    assert x.shape[1] == 2 * T
    f32 = np.float32
    w_in0 = np.asarray(w_in[0], f32)
    groups = [w_in0[:, g * D:(g + 1) * D] for g in range(9)]
    wu = {}
    for par in range(2):
        order = [0, 1, 2, 3, 4, 5, 6, 7, 8] if par == 0 else [0, 1, 3, 2, 4, 5, 6, 7, 8]
        wu[par] = np.concatenate([_units(groups[g], 128) for g in order], axis=0)
    wo = np.asarray(w_out[0], f32)
    uA = _units(wo[:D], 512)
    uB = _units(wo[D:], 512)
    wu_out = np.ascontiguousarray(np.stack([uA, uB], axis=1).reshape(2 * uA.shape[0], 128, -1))
    wu_g = _units(np.asarray(w_ple_gate[0], f32), 512)
    wu_p = _units(np.asarray(w_ple[0], f32), 512)
    bcv = np.stack([np.broadcast_to(np.asarray(norm_w[0], f32), (128, D)),
                    np.broadcast_to(np.asarray(final_norm_w, f32), (128, D))]).astype(f32)
    bcv = np.ascontiguousarray(bcv)
    cst = _consts()
    chv = {}
    for par in range(2):
        do, di = (0, 1) if par == 0 else (1, 0)
        cw = np.asarray(conv_w[0], f32)
        taps = [cw[0], cw[1], cw[2]] if par == 0 else [cw[2], cw[1], cw[0]]
        vecs = [lb_theta[do, 0], lb_theta[do, 1], lb_theta[di, 0], lb_theta[di, 1], hgrn_norm_w[0],
                taps[0], taps[1], taps[2], conv_norm_w[0]]
        chv[par] = np.ascontiguousarray(np.concatenate([_chan(np.asarray(v, f32), NH) for v in vecs], axis=1))
    in_maps = []
    for c in range(2 * B):
        b, par = c // 2, c % 2
        xb = np.asarray(x[b], f32)
        pb = np.asarray(p[0, b], f32)
        if par == 1:
            xb = xb[::-1]
            pb = pb[::-1]
        in_maps.append({
            "xl": np.ascontiguousarray(xb),
            "pl": np.ascontiguousarray(pb[:T]),
            "wu_in": wu[par], "wu_out": wu_out, "wu_g": wu_g, "wu_p": wu_p,
            "chvec": chv[par], "bcv": bcv, "cst": cst,
        })
    return in_maps


def assemble(cfg, results, B):
    T, D = cfg.T, cfg.D
    out = np.empty((B, 2 * T, D), np.float32)
    for c in range(2 * B):
        b, par = c // 2, c % 2
        yc = np.asarray(results[c]["y"], np.float32)
        if par == 0:
            out[b, :T] = yc
        else:
            out[b, T:] = yc[::-1]
    return out


_NC_CACHE = {}


def kernel(x, p, norm_w, w_in, lb_theta, hgrn_norm_w, conv_w, conv_norm_w,
           w_out, w_ple, w_ple_gate, final_norm_w):
    x = np.asarray(x)
    B, S, D = x.shape
    cfg = Cfg(D=D, T=S // 2, PLE=np.asarray(p).shape[-1])
    in_maps = make_in_maps(cfg, x, np.asarray(p), np.asarray(norm_w), np.asarray(w_in), np.asarray(lb_theta),
                           np.asarray(hgrn_norm_w), np.asarray(conv_w), np.asarray(conv_norm_w),
                           np.asarray(w_out), np.asarray(w_ple), np.asarray(w_ple_gate), np.asarray(final_norm_w))
    nc = build(cfg)
    res = run_bass_kernel_spmd(nc, in_maps, core_ids=list(range(2 * B)))
    return assemble(cfg, res.results, B)
```
